# Optimizing a Trainium2 kernel written in Bass

```python
import jax, jax.numpy as jnp
from jax import lax
import numpy as np

D_MODEL = 1024
BATCH = 4
SEQ = 8192
DEPTH = 1

ATTN_WIDTH = D_MODEL // 2
HEAD_DIM = 64
N_HEADS = ATTN_WIDTH // HEAD_DIM
CONV_WIDTH_CH = D_MODEL - ATTN_WIDTH
CONV_GROUPS = CONV_WIDTH_CH // HEAD_DIM
CONV_KERNEL = 31
D_FF = 2816
Q_BLOCK = 128
N_SUBLAYERS = 3
MIX_IN = 3 * ATTN_WIDTH + 2 * CONV_WIDTH_CH
RMS_EPS = 1e-6
LN_EPS = 1e-5

kernel_name = "hybrid_stickbreak_conformer_macaron_block"


def rms_norm(x, g, eps=RMS_EPS):
    xf = x.astype(jnp.float32)
    y = xf * lax.rsqrt(jnp.mean(xf * xf, axis=-1, keepdims=True) + eps)
    return (y * g.astype(jnp.float32)).astype(x.dtype)


def layer_norm(x, g, b, eps=LN_EPS):
    xf = x.astype(jnp.float32)
    mu = jnp.mean(xf, axis=-1, keepdims=True)
    var = jnp.mean(jnp.square(xf - mu), axis=-1, keepdims=True)
    y = (xf - mu) * lax.rsqrt(var + eps)
    return (y * g.astype(jnp.float32) + b.astype(jnp.float32)).astype(x.dtype)


def modulate(h, shift, scale):
    return h * (1.0 + scale[:, None, :]) + shift[:, None, :]


def swiglu_ffn(h, w_in, w_out):
    gate, up = jnp.split(h @ w_in, 2, axis=-1)
    return (jax.nn.silu(gate) * up) @ w_out


def stick_breaking_attention(q, k, v):
    seq = q.shape[2]
    scale = HEAD_DIM ** -0.5
    qf = q.astype(jnp.float32) * scale
    kf = k.astype(jnp.float32)
    vf = v.astype(jnp.float32)
    outs = []
    for start in range(0, seq, Q_BLOCK):
        end = start + Q_BLOCK
        q_blk = qf[:, :, start:end]
        k_ctx = kf[:, :, :end]
        v_ctx = vf[:, :, :end]
        z = jnp.einsum('bhqd,bhkd->bhqk', q_blk, k_ctx)
        q_pos = jnp.arange(start, end)[:, None]
        k_pos = jnp.arange(end)[None, :]
        strict = k_pos < q_pos
        log_one_minus = jnp.where(strict, jax.nn.log_sigmoid(-z), 0.0)
        after = lax.cumsum(log_one_minus, axis=3, reverse=True) - log_one_minus
        log_w = jax.nn.log_sigmoid(z) + after
        w = jnp.where(strict, jnp.exp(log_w), 0.0)
        outs.append(jnp.einsum('bhqk,bhkd->bhqd', w, v_ctx))
    return jnp.concatenate(outs, axis=2).astype(q.dtype)


def causal_depthwise_conv(u, w, b):
    y = lax.conv_general_dilated(
        u, w[:, None, :].astype(u.dtype), window_strides=(1,),
        padding=[(CONV_KERNEL - 1, 0)],
        dimension_numbers=('NWC', 'WIO', 'NWC'),
        feature_group_count=u.shape[-1])
    return y + b


def hybrid_mixer(h, w_in_mix, g_attn_out, conv_w, conv_b, conv_ln_g, conv_ln_b, w_out_mix):
    bsz, seq, _ = h.shape
    proj = h @ w_in_mix
    q, k, v, cv, cg = jnp.split(
        proj, [ATTN_WIDTH, 2 * ATTN_WIDTH, 3 * ATTN_WIDTH, 3 * ATTN_WIDTH + CONV_WIDTH_CH], axis=-1)

    def heads(t):
        return t.reshape(bsz, seq, N_HEADS, HEAD_DIM).transpose(0, 2, 1, 3)

    a = stick_breaking_attention(heads(q), heads(k), heads(v))
    a = rms_norm(a, g_attn_out[:, None, :])
    a = a.transpose(0, 2, 1, 3).reshape(bsz, seq, ATTN_WIDTH)

    u = cv * jax.nn.sigmoid(cg)
    u = causal_depthwise_conv(u, conv_w, conv_b)
    u = jax.nn.silu(layer_norm(u, conv_ln_g, conv_ln_b))

    return jnp.concatenate([a, u], axis=-1) @ w_out_mix


def sandwich_sublayer(x, g_pre, g_post, shift, scale, gate, res_w, fn):
    h = modulate(rms_norm(x, g_pre), shift, scale)
    y = rms_norm(fn(h), g_post)
    return x + res_w * (1.0 + gate[:, None, :]) * y


def setup_inputs(seed: int = 0) -> dict:
    key = jax.random.key(seed)
    ks = jax.random.split(key, 24)
    f32 = jnp.float32

    def nrm(k, shape, s):
        return jax.random.normal(k, shape, f32) * s

    def gain(k, n):
        return 1.0 + 0.02 * jax.random.normal(k, (n,), f32)

    return {
        "x": jax.random.normal(ks[0], (BATCH, SEQ, D_MODEL), f32),
        "c": jax.random.normal(ks[1], (BATCH, D_MODEL), f32),
        "w_ada": nrm(ks[2], (D_MODEL, 3 * N_SUBLAYERS * D_MODEL), 0.1 * D_MODEL ** -0.5),
        "b_ada": nrm(ks[3], (3 * N_SUBLAYERS * D_MODEL,), 0.02),
        "g_pre_ff1": gain(ks[4], D_MODEL),
        "g_post_ff1": gain(ks[5], D_MODEL),
        "ff1_w_in": nrm(ks[6], (D_MODEL, 2 * D_FF), D_MODEL ** -0.5),
        "ff1_w_out": nrm(ks[7], (D_FF, D_MODEL), D_FF ** -0.5),
        "g_pre_mix": gain(ks[8], D_MODEL),
        "g_post_mix": gain(ks[9], D_MODEL),
        "w_in_mix": nrm(ks[10], (D_MODEL, MIX_IN), D_MODEL ** -0.5),
        "g_attn_out": 1.0 + 0.02 * jax.random.normal(ks[11], (N_HEADS, HEAD_DIM), f32),
        "conv_w": nrm(ks[12], (CONV_KERNEL, CONV_WIDTH_CH), CONV_KERNEL ** -0.5),
        "conv_b": nrm(ks[13], (CONV_WIDTH_CH,), 0.02),
        "conv_ln_g": gain(ks[14], CONV_WIDTH_CH),
        "conv_ln_b": nrm(ks[15], (CONV_WIDTH_CH,), 0.02),
        "w_out_mix": nrm(ks[16], (D_MODEL, D_MODEL), D_MODEL ** -0.5),
        "g_pre_ff2": gain(ks[17], D_MODEL),
        "g_post_ff2": gain(ks[18], D_MODEL),
        "ff2_w_in": nrm(ks[19], (D_MODEL, 2 * D_FF), D_MODEL ** -0.5),
        "ff2_w_out": nrm(ks[20], (D_FF, D_MODEL), D_FF ** -0.5),
    }


def reference(x, c, w_ada, b_ada, g_pre_ff1, g_post_ff1, ff1_w_in, ff1_w_out,
              g_pre_mix, g_post_mix, w_in_mix, g_attn_out, conv_w, conv_b,
              conv_ln_g, conv_ln_b, w_out_mix, g_pre_ff2, g_post_ff2,
              ff2_w_in, ff2_w_out):
    mod = (jax.nn.silu(c) @ w_ada + b_ada).reshape(c.shape[0], N_SUBLAYERS, 3, D_MODEL)
    h = x
    for _layer in range(DEPTH):
        h = sandwich_sublayer(
            h, g_pre_ff1, g_post_ff1, mod[:, 0, 0], mod[:, 0, 1], mod[:, 0, 2], 0.5,
            lambda t: swiglu_ffn(t, ff1_w_in, ff1_w_out))
        h = sandwich_sublayer(
            h, g_pre_mix, g_post_mix, mod[:, 1, 0], mod[:, 1, 1], mod[:, 1, 2], 1.0,
            lambda t: hybrid_mixer(t, w_in_mix, g_attn_out, conv_w, conv_b,
                                   conv_ln_g, conv_ln_b, w_out_mix))
        h = sandwich_sublayer(
            h, g_pre_ff2, g_post_ff2, mod[:, 2, 0], mod[:, 2, 1], mod[:, 2, 2], 0.5,
            lambda t: swiglu_ffn(t, ff2_w_in, ff2_w_out))
    return h
```

```python
import os
import numpy as np
import ml_dtypes
from contextlib import ExitStack
import concourse.bass as bass
import concourse.mybir as mybir
from concourse.bass_utils import run_bass_kernel_spmd

F32 = mybir.dt.float32
BF16 = mybir.dt.bfloat16
AF = mybir.ActivationFunctionType
ALU = mybir.AluOpType

D = 1024
KC = 8
S = 8192
DFF = 2816
FC = 22
NOWN = 4096
TT = 256
RMS_EPS = 1e-6
LN_EPS = 1e-5
NEG = -30720.0
ENGS = ("pe", "act", "dve", "pool", "sp")
BLK = {"pe": "tensor", "act": "scalar", "dve": "vector", "pool": "gpsimd", "sp": "sync"}


class _Op:
    __slots__ = ("eng", "fn", "deps", "signal", "sig_idx", "is_dma", "dsem", "dkey", "dval")


class Sched:
    def __init__(self, nc, st, tag):
        self.nc, self.st, self.tag = nc, st, tag
        self.ops = []
        self.last_w = {}
        self.readers = {}
        self.streams = {}
        self.nd = 0

    def _sem(self, name):
        return self.st.enter_context(self.nc.semaphore(f"{self.tag}_{name}"))

    def _add(self, op, reads, writes):
        deps = op.deps
        for k in reads:
            w = self.last_w.get(k)
            if w is not None:
                deps.append((w, 0))
        for k in writes:
            w = self.last_w.get(k)
            if w is not None:
                deps.append((w, 1))
            rd = self.readers.get(k)
            if rd:
                for r in rd.values():
                    deps.append((r, 2))
        for k in writes:
            self.last_w[k] = op
            self.readers[k] = {}
        for k in reads:
            if op.is_dma:
                self.nd += 1
                rk = ("dma", self.nd)
            else:
                rk = op.eng
            self.readers.setdefault(k, {})[rk] = op
        self.ops.append(op)

    def op(self, eng, fn, r=(), w=()):
        o = _Op()
        o.eng, o.fn, o.deps, o.signal, o.sig_idx, o.is_dma = eng, fn, [], False, 0, False
        self._add(o, r, w)
        return o

    def dma(self, stream, slot, nslots, pairs, r=(), w=()):
        stt = self.streams.get(stream)
        if stt is None:
            stt = self.streams[stream] = {
                "sems": [self._sem(f"d{stream}{i}") for i in range(nslots)], "cnt": [0] * nslots}
        o = _Op()
        o.eng, o.deps, o.signal, o.sig_idx, o.is_dma = "sp", [], False, 0, True
        o.fn = lambda eh, pairs=pairs: [eh.dma_start(out=a, in_=b) for (a, b) in pairs]
        o.dsem = stt["sems"][slot]
        o.dkey = (stream, slot)
        stt["cnt"][slot] += 16 * len(pairs)
        o.dval = stt["cnt"][slot]
        self._add(o, r, w)
        return o

    @staticmethod
    def _needs(op, p, kind):
        if p.is_dma:
            return True
        if p.eng == op.eng and not op.is_dma:
            return kind == 0 and p.eng != "pe"
        return True

    def emit(self):
        for op in self.ops:
            for (p, kind) in op.deps:
                if not p.is_dma and self._needs(op, p, kind):
                    p.signal = True
        cnt = {e: 0 for e in ENGS}
        for op in self.ops:
            if not op.is_dma and op.signal:
                cnt[op.eng] += 1
                op.sig_idx = cnt[op.eng]
        esem = {e: self._sem("e" + e) for e in ENGS if cnt[e] > 0}
        per = {e: [op for op in self.ops if op.eng == e] for e in ENGS}
        needs = self._needs
        streams = self.streams
        with self.nc.Block() as block:
            for e in ENGS:
                if not per[e]:
                    continue

                def body(eh, e=e):
                    waited = {}
                    for op in per[e]:
                        need = {}
                        for (p, kind) in op.deps:
                            if not needs(op, p, kind):
                                continue
                            if p.is_dma:
                                key, sem, val = ("d",) + p.dkey, p.dsem, p.dval
                            else:
                                key, sem, val = ("e", p.eng), esem[p.eng], p.sig_idx
                            if key not in need or need[key][1] < val:
                                need[key] = (sem, val)
                        for key, (sem, val) in need.items():
                            if waited.get(key, 0) >= val:
                                continue
                            eh.wait_ge(sem, val)
                            waited[key] = val
                        ins = op.fn(eh)
                        if op.is_dma:
                            for i in ins:
                                i.then_inc(op.dsem, 16)
                        elif op.signal:
                            ins.then_inc(esem[e], 1)
                    if e == "sp":
                        for stt in streams.values():
                            for sem, c in zip(stt["sems"], stt["cnt"]):
                                if c > 0:
                                    eh.wait_ge(sem, c)

                getattr(block, BLK[e])(body)
        return cnt


def _cast(sc, i, dst, src, r, w):
    e = ("dve", "pool", "act")[i % 3]
    if e == "act":
        sc.op("act", lambda eh: eh.activation(out=dst, in_=src, func=AF.Copy), r=r, w=w)
    else:
        sc.op(e, lambda eh: eh.tensor_copy(out=dst, in_=src), r=r, w=w)


def _load_weight(sc, stg, ctr, dst, dst_key, src, total, piece=1024):
    n = (total + piece - 1) // piece
    for i in range(n):
        a, b = i * piece, min(total, (i + 1) * piece)
        sl = ctr[0] % len(stg)
        ctr[0] += 1
        sc.dma("stg", sl, len(stg), [(stg[sl][:, 0:b - a], src[:, a:b])], w=[("stg", sl)])
        _cast(sc, ctr[0], dst[:, a:b], stg[sl][:, 0:b - a], r=[("stg", sl)], w=[dst_key])


def _norm_p1(sc, src, src_key, sq, nchunk):
    sc.op("act", lambda eh: eh.activation(out=sq[:, 0:nchunk, :], in_=src[:, 0:nchunk, :], func=AF.Square),
          r=[src_key], w=["sq"])


def _norm_p2(sc, sq, ones, ps_ss, rs, rs_key, nchunk, ntok, inv_n):
    for k in range(nchunk):
        sc.op("pe", lambda eh, k=k: eh.matmul(ps_ss[:, 0:ntok], lhsT=ones[:, :], rhs=sq[:, k, :],
                                              start=(k == 0), stop=(k == nchunk - 1)),
              r=["sq", "ones"], w=["ps_ss"])
    sc.op("act", lambda eh: eh.activation(out=rs[:, :], in_=ps_ss[:, 0:ntok], func=AF.Sqrt, bias=eps_ap(sc), scale=inv_n),
          r=["ps_ss", "eps"], w=[rs_key])
    sc.op("dve", lambda eh: eh.reciprocal(out=rs[:, :], in_=rs[:, :]), r=[rs_key], w=[rs_key])


def _norm_stats(sc, src, src_key, sq, ones, ps_ss, rs, rs_key, nchunk, ntok, inv_n, eps):
    _norm_p1(sc, src, src_key, sq, nchunk)
    _norm_p2(sc, sq, ones, ps_ss, rs, rs_key, nchunk, ntok, inv_n)


_EPS = {}


def eps_ap(sc):
    return _EPS["rms"]


def phase_ffn(nc, T, cf, which, dbg=None):
    tag = f"F{which}"
    ntiles = (S if which == 1 else NOWN) // TT
    s_idx = 0 if which == 1 else 2
    src_dram = T["xT"] if which == 1 else T["h2"]
    with ExitStack() as st:
        sc = Sched(nc, st, tag)
        sb = lambda n, shp, dt: st.enter_context(nc.sbuf_tensor(f"{tag}_{n}", shp, dt))
        pp = lambda n, shp, dt=F32: st.enter_context(nc.psum_tensor(f"{tag}_p_{n}", shp, dt))
        w_in = sb("w_in", [128, 8 * 2 * DFF], BF16)
        w_out = sb("w_out", [128, FC * D], BF16)
        stg = [sb(f"stg{i}", [128, 1024], F32) for i in range(2)]
        ones = sb("ones", [128, 128], BF16)
        epsb = sb("epsb", [128, 1], F32)
        xT = [sb(f"xT{i}", [128, 8, TT], F32) for i in range(2)]
        sq = sb("sq", [128, 8, TT], BF16)
        t32 = sb("t32", [128, 8, TT], F32)
        hT = [sb(f"hT{i}", [128, 8, TT], BF16) for i in range(2)]
        sg = [sb(f"sg{i}", [128, TT], F32) for i in range(2)]
        actT = sb("actT", [128, FC, TT], BF16)
        yT = sb("yT", [128, 8, TT], F32)
        hm = sb("hm", [128, 8, TT], BF16) if which == 1 else None
        rs1 = [sb(f"rs1{i}", [128, TT], F32) for i in range(2)]
        rs2 = sb("rs2", [128, TT], F32)
        rs3 = rs2
        ps_gu = [pp(f"gu{i}", [128, 512]) for i in range(2)]
        ps_y = [pp(f"y{i}", [128, 512]) for i in range(2)]
        ps_ss = pp("ss", [128, 512])
        ps_md = pp("md", [128, 512])
        _EPS["rms"] = epsb[:, 0:1]

        sc.op("pool", lambda eh: eh.memset(ones[:, :], 1.0), w=["ones"])
        sc.op("pool", lambda eh: eh.memset(epsb[:, :], RMS_EPS), w=["eps"])
        ctr = [0]

        if which == 1:
            scT = sb("scT", [128, 8], F32)
            gv = sb("gv", [128, 48], F32)
            bada = sb("bada", [128, 72], F32)
            modT = sb("modT", [128, 72], F32)
            zpad = sb("zpad", [128, 8, 32], BF16)
            sc.dma("c", 0, 1, [(scT[:, :], T["cT"][:, :]), (gv[:, :], T["gvec"][:, :]),
                               (bada[:, :], T["bada"][:, :])], w=["scT", "gv", "bada"])
            sc.op("act", lambda eh: eh.activation(out=scT[:, :], in_=scT[:, :], func=AF.Silu), r=["scT"], w=["scT"])
            for j in range(72):
                sl = ctr[0] % 2
                ctr[0] += 1
                sc.dma("stg", sl, 2, [(stg[sl][:, :], T["wada"][j, :, :])], w=[("stg", sl)])
                for k in range(8):
                    sc.op("pe", lambda eh, sl=sl, j=j, k=k: eh.matmul(
                        ps_md[:, j:j + 1], lhsT=stg[sl][:, k * 128:(k + 1) * 128], rhs=scT[:, k:k + 1],
                        start=(k == 0), stop=(k == 7)), r=[("stg", sl), "scT"], w=["ps_md"])
            sc.op("dve", lambda eh: eh.tensor_tensor(out=modT[:, :], in0=ps_md[:, 0:72], in1=bada[:, :], op=ALU.add),
                  r=["ps_md", "bada"], w=["modT"])
            for s in range(3):
                resw = 1.0 if s == 1 else 0.5
                A = cf[:, (3 * s) * 8:(3 * s) * 8 + 8]
                SH = cf[:, (3 * s + 1) * 8:(3 * s + 1) * 8 + 8]
                B = cf[:, (3 * s + 2) * 8:(3 * s + 2) * 8 + 8]
                m_sh = modT[:, (3 * s) * 8:(3 * s) * 8 + 8]
                m_sc = modT[:, (3 * s + 1) * 8:(3 * s + 1) * 8 + 8]
                m_g = modT[:, (3 * s + 2) * 8:(3 * s + 2) * 8 + 8]
                gpre = gv[:, (2 * s) * 8:(2 * s) * 8 + 8]
                gpost = gv[:, (2 * s + 1) * 8:(2 * s + 1) * 8 + 8]
                sc.op("dve", lambda eh, A=A, m_sc=m_sc, gpre=gpre: eh.scalar_tensor_tensor(
                    out=A, in0=m_sc, scalar=1.0, in1=gpre, op0=ALU.add, op1=ALU.mult), r=["modT", "gv"], w=["cf"])
                sc.op("dve", lambda eh, SH=SH, m_sh=m_sh: eh.tensor_copy(out=SH, in_=m_sh), r=["modT"], w=["cf"])
                sc.op("dve", lambda eh, B=B, m_g=m_g, gpost=gpost: eh.scalar_tensor_tensor(
                    out=B, in0=m_g, scalar=1.0, in1=gpost, op0=ALU.add, op1=ALU.mult), r=["modT", "gv"], w=["cf"])
                sc.op("dve", lambda eh, B=B, resw=resw: eh.tensor_scalar(
                    out=B, in0=B, scalar1=resw, scalar2=None, op0=ALU.mult), r=["cf"], w=["cf"])
            sc.op("pool", lambda eh: eh.memset(zpad[:, :, :], 0.0), w=["zpad"])
            sc.dma("zp", 0, 1, [(T["hm"][:, :, S:S + 32], zpad[:, :, :])], r=["zpad"])
            if dbg is not None:
                sc.dma("dbg", 0, 1, [(dbg["modT"][:, :], modT[:, :])], r=["modT"])

        if which == 1:
            _load_weight(sc, stg, ctr, w_in, "w_in", T["w1in"], 8 * 2 * DFF)
            _load_weight(sc, stg, ctr, w_out, "w_out", T["w1out"], FC * D)
            stgb = [sb(f"stgb{i}", [128, 512], BF16) for i in range(2)]
            jobs = []
            for (src, dst, tot) in ((T["w2in"], T["w2in_bf"], 8 * 2 * DFF), (T["w2out"], T["w2out_bf"], FC * D),
                                    (T["wmix"], T["wmix_bf"], 8 * 2560), (T["womix"], T["womix_bf"], 8 * D)):
                for a0 in range(0, tot, 512):
                    jobs.append((src[:, a0:a0 + 512], dst[:, a0:a0 + 512]))
            jobs.reverse()

            pc_state = {"n": 0, "load": None, "cast": None}

            def precast_job():
                if pc_state["cast"] is not None:
                    dst, sl = pc_state["cast"]
                    sc.dma("pco", sl, 2, [(dst, stgb[sl][:, :])], r=[("stgb", sl)])
                    pc_state["cast"] = None
                if pc_state["load"] is not None:
                    dst, sl = pc_state["load"]
                    sc.op("act", lambda eh, sl=sl: eh.activation(out=stgb[sl][:, :], in_=stg[sl][:, 0:512], func=AF.Copy),
                          r=[("stg", sl)], w=[("stgb", sl)])
                    pc_state["cast"] = (dst, sl)
                    pc_state["load"] = None
                if jobs:
                    src, dst = jobs.pop()
                    sl = pc_state["n"] % 2
                    pc_state["n"] += 1
                    sc.dma("stg", sl, 2, [(stg[sl][:, 0:512], src)], w=[("stg", sl)])
                    pc_state["load"] = (dst, sl)

            def precast_pending():
                return bool(jobs) or pc_state["load"] is not None or pc_state["cast"] is not None
        else:
            for i_ in range(4):
                a0, b0 = i_ * 11264, (i_ + 1) * 11264
                sc.dma("wld", i_, 4, [(w_in[:, a0:b0], T["w2in_bf"][:, a0:b0])], w=["w_in"])
            for i_ in range(2):
                a0, b0 = i_ * 11264, (i_ + 1) * 11264
                sc.dma("wld2", i_, 2, [(w_out[:, a0:b0], T["w2out_bf"][:, a0:b0])], w=["w_out"])

            def precast_job():
                return

        cA = lambda s, k: cf[:, (3 * s) * 8 + k:(3 * s) * 8 + k + 1]
        cS = lambda s, k: cf[:, (3 * s + 1) * 8 + k:(3 * s + 1) * 8 + k + 1]
        cB = lambda s, k: cf[:, (3 * s + 2) * 8 + k:(3 * s + 2) * 8 + k + 1]

        YT = [("yT", k) for k in range(8)]

        def mod_pair(src, src_keys, rs, rs_key, s, dst, dst_key, k):
            sc.op("dve", lambda eh: eh.scalar_tensor_tensor(
                out=t32[:, k, :], in0=src[:, k, :], scalar=cA(s, k), in1=rs[:, :], op0=ALU.mult, op1=ALU.mult),
                r=[src_keys[k], rs_key, "cf"], w=[("t32", k)])
            sc.op("act", lambda eh: eh.activation(
                out=dst[:, k, :], in_=t32[:, k, :], func=AF.Identity, bias=cS(s, k), scale=1.0),
                r=[("t32", k), "cf"], w=[dst_key])

        def load_x(i):
            b = i % 2
            sc.dma("x", b, 2, [(xT[b][:, :, :], src_dram[:, :, i * TT:(i + 1) * TT])], w=[("xT", b)])

        def stage_P1(i):
            b = i % 2
            _norm_p1(sc, xT[b], ("xT", b), sq, 8)

        def stage_P2(i):
            b = i % 2
            _norm_p2(sc, sq, ones, ps_ss, rs1[b], ("rs1", b), 8, TT, 1.0 / D)

        def stage_P3(i, ks):
            b = i % 2
            for k in ks:
                mod_pair(xT[b], [("xT", b)] * 8, rs1[b], ("rs1", b), s_idx, hT[b], ("hT", b), k)

        def stage_G(i, hooks):
            b = i % 2
            for f in range(FC):
                g = f % 2
                for half in range(2):
                    col = half * DFF + f * 128
                    for k in range(8):
                        sc.op("pe", lambda eh, g=g, half=half, col=col, k=k: eh.matmul(
                            ps_gu[g][:, half * TT:(half + 1) * TT],
                            lhsT=w_in[:, k * 2 * DFF + col:k * 2 * DFF + col + 128], rhs=hT[b][:, k, :],
                            start=(k == 0), stop=(k == 7)), r=["w_in", ("hT", b)], w=[("ps_gu", g)])
                sc.op("act", lambda eh, g=g: eh.activation(out=sg[g][:, :], in_=ps_gu[g][:, 0:TT], func=AF.Silu),
                      r=[("ps_gu", g)], w=[("sg", g)])
                sc.op("dve", lambda eh, g=g, f=f: eh.tensor_tensor(
                    out=actT[:, f, :], in0=sg[g][:, :], in1=ps_gu[g][:, TT:2 * TT], op=ALU.mult),
                    r=[("sg", g), ("ps_gu", g)], w=["actT"])
                for h in hooks.get(f, ()):
                    h()

        def stage_O1(i):
            for d in range(8):
                g = d % 2
                for f in range(FC):
                    sc.op("pe", lambda eh, g=g, d=d, f=f: eh.matmul(
                        ps_y[g][:, 0:TT], lhsT=w_out[:, f * D + d * 128:f * D + d * 128 + 128], rhs=actT[:, f, :],
                        start=(f == 0), stop=(f == FC - 1)), r=["w_out", "actT"], w=[("ps_y", g)])
                sc.op("act", lambda eh, g=g, d=d: eh.activation(out=yT[:, d, :], in_=ps_y[g][:, 0:TT], func=AF.Copy),
                      r=[("ps_y", g)], w=[("yT", d)])

        def norm_p1_yT():
            sc.op("act", lambda eh: eh.activation(out=sq[:, :, :], in_=yT[:, :, :], func=AF.Square), r=YT, w=["sq"])

        def stage_O2a1(i):
            norm_p1_yT()

        def stage_O2a2(i):
            _norm_p2(sc, sq, ones, ps_ss, rs2, "rs2", 8, TT, 1.0 / D)

        def stage_O2a3(i, ks):
            b = i % 2
            for k in ks:
                sc.op("dve", lambda eh, k=k: eh.scalar_tensor_tensor(
                    out=t32[:, k, :], in0=yT[:, k, :], scalar=cB(s_idx, k), in1=rs2[:, :], op0=ALU.mult, op1=ALU.mult),
                    r=[("yT", k), "rs2", "cf"], w=[("t32", k)])
                sc.op("pool", lambda eh, k=k: eh.tensor_tensor(
                    out=yT[:, k, :], in0=t32[:, k, :], in1=xT[b][:, k, :], op=ALU.add),
                    r=[("t32", k), ("xT", b)], w=[("yT", k)])
            if 7 in ks:
                if which == 1:
                    sc.dma("h1", 0, 1, [(T["h1"][:, :, i * TT:(i + 1) * TT], yT[:, :, :])], r=YT)
                else:
                    sc.dma("out", 0, 1, [(T["outT"][:, :, i * TT:(i + 1) * TT], yT[:, :, :])], r=YT)

        def stage_O2b1(i):
            if which == 1:
                norm_p1_yT()

        def stage_O2b2(i):
            if which == 1:
                _norm_p2(sc, sq, ones, ps_ss, rs3, "rs2", 8, TT, 1.0 / D)

        def stage_O2b3(i, ks):
            if which != 1:
                return
            for k in ks:
                mod_pair(yT, YT, rs3, "rs2", 1, hm, "hm", k)
            if 7 in ks:
                sc.dma("hm", 0, 1, [(T["hm"][:, :, i * TT:(i + 1) * TT], hm[:, :, :])], r=["hm"])

        PAIRS = [(0, 1), (2, 3), (4, 5), (6, 7)]

        def tile_hooks(i):
            hooks = {}
            add = lambda f, fn: hooks.setdefault(f, []).append(fn)
            if i >= 1:
                add(0, lambda: stage_O2a1(i - 1))
                add(2, lambda: stage_O2a2(i - 1))
                for n_, ks in enumerate(PAIRS):
                    add(3 + n_, lambda ks=ks: stage_O2a3(i - 1, ks))
                add(7, lambda: stage_O2b1(i - 1))
                add(9, lambda: stage_O2b2(i - 1))
                for n_, ks in enumerate(PAIRS):
                    add(10 + n_, lambda ks=ks: stage_O2b3(i - 1, ks))
                if i + 1 < ntiles:
                    add(7, lambda: load_x(i + 1))
            for f_ in (1, 5, 8, 11, 13, 19):
                add(f_, precast_job)
            if i + 1 < ntiles:
                add(14, lambda: stage_P1(i + 1))
                add(16, lambda: stage_P2(i + 1))
                for n_, ks in enumerate(PAIRS):
                    add(17 + n_, lambda ks=ks: stage_P3(i + 1, ks))
            return hooks

        load_x(0)
        load_x(1)
        stage_P1(0)
        stage_P2(0)
        stage_P3(0, range(8))
        for w_ in range(48):
            sc.op("pe", lambda eh: eh.matmul(ps_md[:, :], lhsT=w_in[:, 512:640], rhs=w_in[:, 0:512], start=True, stop=True),
                  r=["w_in"], w=["ps_md"])
        for i in range(ntiles):
            stage_G(i, tile_hooks(i))
            stage_O1(i)
        while which == 1 and precast_pending():
            precast_job()
        last = ntiles - 1
        stage_O2a1(last)
        stage_O2a2(last)
        stage_O2a3(last, range(8))
        stage_O2b1(last)
        stage_O2b2(last)
        stage_O2b3(last, range(8))
        cnt = sc.emit()
        return cnt


def phase_inproj(nc, T, cf, dbg=None):
    tag = "B"
    NT = 512
    with ExitStack() as st:
        sc = Sched(nc, st, tag)
        sb = lambda n, shp, dt: st.enter_context(nc.sbuf_tensor(f"{tag}_{n}", shp, dt))
        pp = lambda n, shp, dt=F32: st.enter_context(nc.psum_tensor(f"{tag}_p_{n}", shp, dt))
        wmix = sb("wmix", [128, 8 * 2560], BF16)
        idf = sb("idf", [128, 128], F32)
        idb = sb("idb", [128, 128], BF16)
        ones32 = sb("ones32", [128, 128], F32)
        diag = sb("diag", [128, 124, 128], BF16)
        cw = sb("cw", [128, 124], F32)
        cvec = sb("cvec", [128, 12], F32)
        selv = sb("selv", [128, 4], F32)
        epsb = sb("epsb", [128, 1], F32)
        hmb = [sb(f"hmb{i}", [128, 8, NT], BF16) for i in range(2)]
        kt_sb = [sb(f"kt{i}", [128, 4, NT], BF16) for i in range(2)]
        v_sb = [sb(f"v{i}", [128, 4, 512], BF16) for i in range(2)]
        hmx = sb("hmx", [128, 8, 1056], BF16)
        htmp = sb("htmp", [128, 8, 544], BF16)
        hown = sb("hown", [128, 8, 544], BF16)
        qt_sb = sb("qt", [128, 4, NT], BF16)
        sgm = [sb(f"sgm{i}", [128, 544], F32) for i in range(2)]
        ub = [sb(f"u{i}", [128, 4, 544], BF16) for i in range(2)]
        cT = sb("cT", [128, 4, NT], F32)
        csq = sb("csq", [128, 4, NT], F32)
        mean_sb = sb("mean", [128, NT], F32)
        var_sb = sb("var", [128, NT], F32)
        tln = sb("tln", [128, 4, NT], F32)
        u2 = sb("u2", [128, 4, NT], BF16)
        ps_a = [pp(f"a{i}", [128, 512]) for i in range(3)]
        ps_h = [pp(f"h{i}", [128, 512]) for i in range(2)]
        ps_m = pp("m", [128, 512])
        ps_q = pp("q", [128, 512])

        sc.op("pool", lambda eh: eh.memset(ones32[:, :], 1.0 / 512.0), w=["ones32"])
        sc.op("pool", lambda eh: eh.memset(epsb[:, :], LN_EPS), w=["eps"])
        sc.dma("c", 0, 1, [(idf[:, :], T["ident"][:, :]), (cw[:, :], T["convw"][:, :]),
                           (cvec[:, :], T["cvec"][:, :]), (selv[:, :], T["selv"][:, :])],
               w=["idf", "cw", "cvec", "selv"])
        sc.op("dve", lambda eh: eh.tensor_copy(out=idb[:, :], in_=idf[:, :]), r=["idf"], w=["idb"])
        for i in range(124):
            if i % 2 == 0:
                sc.op("dve", lambda eh, i=i: eh.tensor_scalar(out=diag[:, i, :], in0=idf[:, :], scalar1=cw[:, i:i + 1],
                                                              scalar2=None, op0=ALU.mult), r=["idf", "cw"], w=["diag"])
            else:
                sc.op("act", lambda eh, i=i: eh.activation(out=diag[:, i, :], in_=idf[:, :], func=AF.Copy,
                                                           scale=cw[:, i:i + 1]), r=["idf", "cw"], w=["diag"])
        for i_ in range(4):
            a0, b0 = i_ * 5120, (i_ + 1) * 5120
            sc.dma("wld", i_, 4, [(wmix[:, a0:b0], T["wmix_bf"][:, a0:b0])], w=["wmix"])
        WM = lambda k, c0: wmix[:, k * 2560 + c0:k * 2560 + c0 + 128]
        ev = [0]

        def evac(dst, src, r, w, scale=None):
            ev[0] += 1
            if ev[0] % 2 == 0:
                sc.op("act", lambda eh: eh.activation(out=dst, in_=src, func=AF.Copy,
                                                      scale=(1.0 if scale is None else scale)), r=r, w=w)
            elif scale is None:
                sc.op("dve", lambda eh: eh.tensor_copy(out=dst, in_=src), r=r, w=w)
            else:
                sc.op("dve", lambda eh: eh.tensor_scalar(out=dst, in0=src, scalar1=scale, scalar2=None, op0=ALU.mult),
                      r=r, w=w)

        def load_hm(i):
            b = i % 2
            sc.dma("hm", b, 2, [(hmb[b][:, :, :], T["hm"][:, :, i * NT:(i + 1) * NT])], w=[("hmb", b)])

        load_hm(0)
        pa = [0]
        for i in range(16):
            b = i % 2
            if i + 1 < 16:
                load_hm(i + 1)
            for hp in range(4):
                a = pa[0] % 3
                pa[0] += 1
                for k in range(8):
                    sc.op("pe", lambda eh, a=a, hp=hp, k=k, b=b: eh.matmul(
                        ps_a[a][:, :], lhsT=WM(k, 512 + hp * 128), rhs=hmb[b][:, k, :], start=(k == 0), stop=(k == 7)),
                        r=["wmix", ("hmb", b)], w=[("ps_a", a)])
                evac(kt_sb[b][:, hp, :], ps_a[a][:, :], r=[("ps_a", a)], w=[("kt", b)])
            sc.dma("kt", b, 2, [(T["KT"][:, :, i * NT:(i + 1) * NT], kt_sb[b][:, :, :])], r=[("kt", b)])
            for tb in range(4):
                a = pa[0] % 3
                pa[0] += 1
                for k in range(8):
                    sc.op("pe", lambda eh, a=a, tb=tb, k=k, b=b: eh.matmul(
                        ps_a[a][:, :], lhsT=hmb[b][:, k, tb * 128:(tb + 1) * 128],
                        rhs=wmix[:, k * 2560 + 1024:k * 2560 + 1536], start=(k == 0), stop=(k == 7)),
                        r=["wmix", ("hmb", b)], w=[("ps_a", a)])
                evac(v_sb[b][:, tb, :], ps_a[a][:, :], r=[("ps_a", a)], w=[("v", b)])
            sc.dma("v", b, 2, [(T["V"][:, hp, 4 * i:4 * i + 4, :], v_sb[b][:, :, hp * 128:(hp + 1) * 128])
                               for hp in range(4)], r=[("v", b)])

        def X(j):
            u = ub[j % 2]
            uk = ("u", j % 2)
            par = j % 2
            sL = selv[:, 2 * par:2 * par + 1]
            sH = selv[:, 2 * par + 1:2 * par + 2]
            sc.dma("hmx", 0, 1, [(hmx[:, :, :], T["hm"][:, :, 1024 * j:1024 * j + 1056])], w=["hmx"])
            sc.op("act", lambda eh, sL=sL: eh.activation(out=htmp[:, :, :], in_=hmx[:, :, 0:544], func=AF.Copy, scale=sL),
                  r=["hmx", "selv"], w=["htmp"])
            sc.op("dve", lambda eh, sH=sH: eh.scalar_tensor_tensor(
                out=hown[:, :, :], in0=hmx[:, :, 512:1056], scalar=sH, in1=htmp[:, :, :], op0=ALU.mult, op1=ALU.add),
                r=["hmx", "htmp", "selv"], w=["hown"])
            for hp in range(4):
                for k in range(8):
                    sc.op("pe", lambda eh, hp=hp, k=k: eh.matmul(
                        ps_q[:, :], lhsT=WM(k, hp * 128), rhs=hown[:, k, 0:512], start=(k == 0), stop=(k == 7)),
                        r=["wmix", "hown"], w=["ps_q"])
                evac(qt_sb[:, hp, :], ps_q[:, :], r=["ps_q"], w=["qt"], scale=0.125)
            sc.dma("qt", 0, 1, [(T["QT"][:, :, 512 * j:512 * j + 512], qt_sb[:, :, :])], r=["qt"])
            for cc in range(4):
                g = cc % 2
                for (c0, n, off) in ((0, 512, 0), (512, 32, 0)):
                    pv = ps_a[0] if n == 512 else ps_a[2]
                    pg = ps_a[1] if n == 512 else ps_a[2]
                    o2 = 0 if n == 512 else 64
                    for k in range(8):
                        sc.op("pe", lambda eh, pv=pv, cc=cc, k=k, c0=c0, n=n: eh.matmul(
                            pv[:, 0:n], lhsT=WM(k, 1536 + cc * 128), rhs=hown[:, k, c0:c0 + n],
                            start=(k == 0), stop=(k == 7)), r=["wmix", "hown"], w=[("ps_a", 0 if n == 512 else 2)])
                    for k in range(8):
                        sc.op("pe", lambda eh, pg=pg, cc=cc, k=k, c0=c0, n=n, o2=o2: eh.matmul(
                            pg[:, o2:o2 + n], lhsT=WM(k, 2048 + cc * 128), rhs=hown[:, k, c0:c0 + n],
                            start=(k == 0), stop=(k == 7)), r=["wmix", "hown"], w=[("ps_a", 1 if n == 512 else 2)])
                    sc.op("act", lambda eh, pg=pg, g=g, c0=c0, n=n, o2=o2: eh.activation(
                        out=sgm[g][:, c0:c0 + n], in_=pg[:, o2:o2 + n], func=AF.Sigmoid),
                        r=[("ps_a", 1 if n == 512 else 2)], w=[("sgm", g)])
                    sc.op("dve", lambda eh, pv=pv, g=g, cc=cc, c0=c0, n=n: eh.tensor_tensor(
                        out=u[:, cc, c0:c0 + n], in0=sgm[g][:, c0:c0 + n], in1=pv[:, 0:n], op=ALU.mult),
                        r=[("sgm", g), ("ps_a", 0 if n == 512 else 2)], w=[uk])

        def Y(j):
            u = ub[j % 2]
            uk = ("u", j % 2)
            for cc in range(4):
                h = cc % 2
                for k in range(31):
                    sc.op("pe", lambda eh, h=h, cc=cc, k=k: eh.matmul(
                        ps_h[h][:, :], lhsT=diag[:, cc * 31 + k, :], rhs=u[:, cc, 30 - k:30 - k + 512],
                        start=(k == 0), stop=(k == 30)), r=["diag", uk], w=[("ps_h", h)])
                sc.op("act", lambda eh, h=h, cc=cc: eh.activation(
                    out=cT[:, cc, :], in_=ps_h[h][:, :], func=AF.Identity, bias=cvec[:, cc:cc + 1], scale=1.0),
                    r=[("ps_h", h), "cvec"], w=["cT"])
            sc.op("act", lambda eh: eh.activation(out=csq[:, :, :], in_=cT[:, :, :], func=AF.Square), r=["cT"], w=["csq"])
            for cc in range(4):
                sc.op("pe", lambda eh, cc=cc: eh.matmul(ps_m[:, :], lhsT=ones32[:, :], rhs=cT[:, cc, :],
                                                        start=(cc == 0), stop=(cc == 3)), r=["ones32", "cT"], w=["ps_m"])
            sc.op("act", lambda eh: eh.activation(out=mean_sb[:, :], in_=ps_m[:, :], func=AF.Copy), r=["ps_m"], w=["mean"])
            for cc in range(4):
                sc.op("pe", lambda eh, cc=cc: eh.matmul(ps_q[:, :], lhsT=ones32[:, :], rhs=csq[:, cc, :],
                                                        start=(cc == 0), stop=(cc == 3)), r=["ones32", "csq"], w=["ps_q"])
            sc.op("dve", lambda eh: eh.tensor_tensor(out=var_sb[:, :], in0=mean_sb[:, :], in1=mean_sb[:, :], op=ALU.mult),
                  r=["mean"], w=["var"])
            sc.op("dve", lambda eh: eh.tensor_tensor(out=var_sb[:, :], in0=ps_q[:, :], in1=var_sb[:, :], op=ALU.subtract),
                  r=["ps_q", "var"], w=["var"])
            sc.op("act", lambda eh: eh.activation(out=var_sb[:, :], in_=var_sb[:, :], func=AF.Sqrt, bias=epsb[:, 0:1], scale=1.0),
                  r=["var", "eps"], w=["var"])
            sc.op("dve", lambda eh: eh.reciprocal(out=var_sb[:, :], in_=var_sb[:, :]), r=["var"], w=["var"])
            for cc in range(4):
                sc.op("pool", lambda eh, cc=cc: eh.tensor_tensor(out=tln[:, cc, :], in0=cT[:, cc, :], in1=mean_sb[:, :],
                                                                 op=ALU.subtract), r=["cT", "mean"], w=["tln"])
                sc.op("dve", lambda eh, cc=cc: eh.tensor_tensor(out=tln[:, cc, :], in0=tln[:, cc, :], in1=var_sb[:, :],
                                                                op=ALU.mult), r=["tln", "var"], w=["tln"])
                sc.op("act", lambda eh, cc=cc: eh.activation(
                    out=u2[:, cc, :], in_=tln[:, cc, :], func=AF.Silu, bias=cvec[:, 8 + cc:9 + cc], scale=cvec[:, 4 + cc:5 + cc]),
                    r=["tln", "cvec"], w=["u2"])
            sc.dma("u2", 0, 1, [(T["U2T"][:, :, 512 * j:512 * j + 512], u2[:, :, :])], r=["u2"])

        X(0)
        for j in range(8):
            if j + 1 < 8:
                X(j + 1)
            Y(j)
        return sc.emit()


def phase_attn(nc, T, dbg=None, hp_list=(0, 1, 2, 3), slot_list=tuple(range(8))):
    tag = "C"
    with ExitStack() as st:
        sc = Sched(nc, st, tag)
        sb = lambda n, shp, dt: st.enter_context(nc.sbuf_tensor(f"{tag}_{n}", shp, dt))
        pp = lambda n, shp, dt=F32: st.enter_context(nc.psum_tensor(f"{tag}_p_{n}", shp, dt))
        idf = sb("idf", [128, 128], F32)
        idb = sb("idb", [128, 128], BF16)
        QT = sb("QT", [128, 4, NOWN], BF16)
        msk = sb("msk", [128, 16, 512], BF16)
        KT = [sb(f"KT{i}", [128, S], BF16) for i in range(2)]
        V = [sb(f"V{i}", [128, 64, 128], BF16) for i in range(2)]
        NB = 3
        ring = [sb(f"ring{i}", [128, 1 + 16 * 512], F32) for i in range(2)]
        dummy = sb("dummy", [128, 512], F32)
        wb = [[sb(f"w{i}_{e}", [128, 512], BF16) for e in range(2)] for i in range(NB)]
        wT = [[sb(f"wT{i}_{e}", [128, 512], BF16) for e in range(2)] for i in range(2)]
        o_sb = [sb(f"o{i}", [128, 512], F32) for i in range(2)]
        ps_z = [[pp(f"z{i}_{e}", [128, 512]) for e in range(2)] for i in range(2)]
        ps_t = [pp(f"t{e}", [128, 512]) for e in range(2)]
        ps_o = [pp(f"o{i}", [128, 512]) for i in range(2)]

        sc.dma("c", 0, 1, [(idf[:, :], T["ident"][:, :]), (msk[:, :, :], T["masks"][:, :, :]),
                           (QT[:, :, :], T["QT"][:, :, :])], w=["idf", "msk", "QT"])
        sc.op("dve", lambda eh: eh.tensor_copy(out=idb[:, :], in_=idf[:, :]), r=["idf"], w=["idb"])
        for e_ in range(2):
            sc.op("pool", lambda eh, e_=e_: eh.memset(ring[e_][:, 0:1], 1.0), w=[("I0", e_)])
        sc.op("pool", lambda eh: eh.memset(dummy[:, :], 1.0), w=["dummy"])

        def load_kv(hp, b=0):
            sc.dma("kv", b, 2, [(KT[b][:, :], T["KT"][:, hp, :]), (V[b][:, :, :], T["V"][:, hp, :, :])],
                   w=[("KT", b), ("V", b)])

        steps = []
        qctr = 0
        for hi, hp in enumerate(hp_list):
            for j in slot_list:
                for r in range(4):
                    nch = 16 - 2 * j
                    for c in range(nch):
                        steps.append(dict(hp=hp, hi=hi, j=j, r=r, c=c, nch=nch, q=qctr,
                                          first_hp=(j == slot_list[0] and r == 0 and c == 0)))
                    qctr += 1
        N = len(steps)
        for n, it in enumerate(steps):
            it["n"] = n

        def geom(r, c):
            k0 = 128 * r if c == 0 else 0
            W = 512 - k0
            lo = 0 if c == 0 else (512 - 128 * r) + 512 * (c - 1)
            return k0, W, lo

        def S0(it):
            n, hp, j, r, c = it["n"], it["hp"], it["j"], it["r"], it["c"]
            kb = it["hi"] % 2
            zb = n % 2
            kt = 2 * j + c
            q0 = 512 * j + 128 * r
            k0, W, lo = geom(r, c)
            masked = c < 2
            for e in range(2):
                sc.op("pe", lambda eh, e=e: eh.matmul(
                    ps_z[zb][e][:, 0:W], lhsT=QT[e * 64:(e + 1) * 64, hp, q0:q0 + 128],
                    rhs=KT[kb][e * 64:(e + 1) * 64, kt * 512 + k0:(kt + 1) * 512], start=True, stop=(not masked)),
                    r=["QT", ("KT", kb)], w=[("ps_z", zb, e)])
            if masked:
                mi = ((j % 2) * 2 + c) * 4 + r
                for e in range(2):
                    sc.op("pe", lambda eh, e=e: eh.matmul(ps_z[zb][e][:, 0:W], lhsT=idb[:, :], rhs=msk[:, mi, k0:512],
                                                          start=False, stop=True),
                          r=["idb", "msk"], w=[("ps_z", zb, e)])
            for e in range(2):
                sc.op("act", lambda eh, e=e: eh.activation(out=ps_z[zb][e][:, 0:W], in_=ps_z[zb][e][:, 0:W], func=AF.Sigmoid,
                                                           scale=-1.0), r=[("ps_z", zb, e)], w=[("ps_z", zb, e)])

        def S1(it):
            n, r, c = it["n"], it["r"], it["c"]
            b = n % NB
            zb = n % 2
            k0, W, lo = geom(r, c)
            for e in range(2):
                rg = ring[e]
                prev = [("I", e, c - 1)] if c > 0 else [("I0", e)]
                init = 1.0 if c == 0 else rg[:, lo:lo + 1]
                sc.op("dve", lambda eh, e=e, rg=rg, init=init: eh.tensor_tensor_scan(
                    out=rg[:, lo + 1:lo + 1 + W], data0=ps_z[zb][e][:, 0:W], data1=dummy[:, 0:W], initial=init,
                    op0=ALU.mult, op1=ALU.bypass), r=[("ps_z", zb, e), "dummy"] + prev, w=[("I", e, c)])
                sc.op("pool", lambda eh, e=e, rg=rg: eh.tensor_tensor(
                    out=wb[b][e][:, k0:512], in0=rg[:, lo:lo + W], in1=rg[:, lo + 1:lo + 1 + W], op=ALU.subtract),
                    r=[("I", e, c)] + prev, w=[("w", b, e)])

        def S2(it):
            n, r, c = it["n"], it["r"], it["c"]
            b = n % NB
            t = n % 2
            i0 = r if c == 0 else 0
            for e in range(2):
                for i in range(i0, 4):
                    sc.op("pe", lambda eh, e=e, i=i: eh.matmul(
                        ps_t[e][:, i * 128:(i + 1) * 128], lhsT=wb[b][e][:, i * 128:(i + 1) * 128], rhs=idb[:, :],
                        start=True, stop=True), r=[("w", b, e), "idb"], w=[("ps_t", e)])
            for e in range(2):
                sc.op("act", lambda eh, e=e: eh.activation(out=wT[t][e][:, i0 * 128:512], in_=ps_t[e][:, i0 * 128:512],
                                                           func=AF.Copy), r=[("ps_t", e)], w=[("wT", t, e)])

        def S3(it):
            n, hp, j, r, c, nch = it["n"], it["hp"], it["j"], it["r"], it["c"], it["nch"]
            kb = it["hi"] % 2
            t = n % 2
            ob = it["q"] % 2
            kt = 2 * j + c
            i0 = r if c == 0 else 0
            for e in range(2):
                for i in range(i0, 4):
                    sc.op("pe", lambda eh, e=e, i=i: eh.matmul(
                        ps_o[ob][:, e * 128:(e + 1) * 128], lhsT=V[kb][:, kt * 4 + i, :],
                        rhs=wT[t][e][:, i * 128:(i + 1) * 128], start=(e == 0 and c == 0 and i == i0),
                        stop=(c == nch - 1 and i == 3), skip_group_check=True),
                        r=[("V", kb), ("wT", t, e)], w=[("ps_o", ob)])
            if c == nch - 1:
                osl = (it["q"] // 4) % 2
                for e in range(2):
                    sc.op("act", lambda eh, e=e: eh.activation(
                        out=o_sb[osl][e * 64:(e + 1) * 64, r * 128:(r + 1) * 128],
                        in_=ps_o[ob][e * 64:(e + 1) * 64, e * 128:(e + 1) * 128], func=AF.Copy),
                        r=[("ps_o", ob)], w=[("o_sb", osl)])
                if r == 3:
                    sc.dma("ar", osl, 2, [(T["AR"][:, hp, 512 * j:512 * j + 512], o_sb[osl][:, :])], r=[("o_sb", osl)])

        load_kv(hp_list[0])
        for n in range(N + 3):
            if 0 <= n - 3 < N and steps[n - 3]["first_hp"] and steps[n - 3]["hi"] + 1 < len(hp_list):
                nh = steps[n - 3]["hi"] + 1
                sc.dma("kv", nh % 2, 2, [(KT[nh % 2][:, :], T["KT"][:, hp_list[nh], :]),
                                         (V[nh % 2][:, :, :], T["V"][:, hp_list[nh], :, :])],
                       w=[("KT", nh % 2), ("V", nh % 2)])
            if n < N:
                S0(steps[n])
            if 0 <= n - 1 < N:
                S1(steps[n - 1])
            if 0 <= n - 2 < N:
                S2(steps[n - 2])
            if 0 <= n - 3 < N:
                S3(steps[n - 3])
        return sc.emit()


def phase_outproj(nc, T, cf, dbg=None):
    tag = "E"
    NT = 512
    with ExitStack() as st:
        sc = Sched(nc, st, tag)
        sb = lambda n, shp, dt: st.enter_context(nc.sbuf_tensor(f"{tag}_{n}", shp, dt))
        pp = lambda n, shp, dt=F32: st.enter_context(nc.psum_tensor(f"{tag}_p_{n}", shp, dt))
        womix = sb("womix", [128, 8 * D], BF16)
        stg = [sb(f"stg{i}", [128, 1024], F32) for i in range(2)]
        ones = sb("ones", [128, 128], BF16)
        bd32 = sb("bd32", [128, 128], F32)
        epsb = sb("epsb", [128, 1], F32)
        gat = sb("gat", [128, 4], F32)
        selv = sb("selv", [128, 4], F32)
        ar2 = [sb(f"ar{i}", [128, 4, NT], F32) for i in range(2)]
        asq = sb("asq", [128, 4, NT], F32)
        rr = sb("rr", [128, NT], F32)
        cat2 = [sb(f"cat{i}", [128, 8, NT], BF16) for i in range(2)]
        h1L2 = [sb(f"h1L{i}", [128, 8, NT], F32) for i in range(2)]
        h1H2 = [sb(f"h1H{i}", [128, 8, NT], F32) for i in range(2)]
        yT2 = [sb(f"yT{i}", [128, 8, NT], F32) for i in range(2)]
        sq = sb("sq", [128, 8, NT], BF16)
        t322 = [sb(f"t32{i}", [128, 8, NT], F32) for i in range(2)]
        rs = sb("rs", [128, NT], F32)
        ps_n = [pp(f"n{i}", [128, 512]) for i in range(2)]
        ps_y = [pp(f"y{i}", [128, 512]) for i in range(2)]
        ps_ss = pp("ss", [128, 512])
        _EPS["rms"] = epsb[:, 0:1]
        sc.op("pool", lambda eh: eh.memset(ones[:, :], 1.0), w=["ones"])
        sc.op("pool", lambda eh: eh.memset(epsb[:, :], RMS_EPS), w=["eps"])
        sc.dma("c", 0, 1, [(bd32[:, :], T["bd"][:, :]), (gat[:, :], T["gattn"][:, :]), (selv[:, :], T["selv"][:, :])],
               w=["bd32", "gat", "selv"])
        for i_ in range(2):
            a0, b0 = i_ * 4096, (i_ + 1) * 4096
            sc.dma("wld", i_, 2, [(womix[:, a0:b0], T["womix_bf"][:, a0:b0])], w=["womix"])
        cB = lambda k: cf[:, (3 * 1 + 2) * 8 + k:(3 * 1 + 2) * 8 + k + 1]
        def load_in(j):
            q = j % 2
            sc.dma("in", q, 2, [(ar2[q][:, :, :], T["AR"][:, :, NT * j:NT * j + NT]),
                                (cat2[q][:, 4:8, :], T["U2T"][:, :, NT * j:NT * j + NT]),
                                (h1L2[q][:, :, :], T["h1"][:, :, 1024 * j:1024 * j + 512]),
                                (h1H2[q][:, :, :], T["h1"][:, :, 1024 * j + 512:1024 * j + 1024])],
                   w=[("ar", q), ("catu", q), ("h1L", q), ("h1H", q)])

        def H(j):
            par = j % 2
            q = j % 2
            ar, cat, yT = ar2[q], cat2[q], yT2[q]
            sc.op("act", lambda eh, ar=ar: eh.activation(out=asq[:, :, :], in_=ar[:, :, :], func=AF.Square), r=[("ar", q)], w=["asq"])
            for hp in range(4):
                g = hp % 2
                sc.op("pe", lambda eh, g=g, hp=hp: eh.matmul(ps_n[g][:, :], lhsT=bd32[:, :], rhs=asq[:, hp, :],
                                                             start=True, stop=True), r=["bd32", "asq"], w=[("ps_n", g)])
                sc.op("act", lambda eh, g=g: eh.activation(out=rr[:, :], in_=ps_n[g][:, :], func=AF.Sqrt,
                                                           bias=epsb[:, 0:1], scale=1.0), r=[("ps_n", g), "eps"], w=["rr"])
                sc.op("dve", lambda eh: eh.reciprocal(out=rr[:, :], in_=rr[:, :]), r=["rr"], w=["rr"])
                sc.op("dve", lambda eh, hp=hp, cat=cat, ar=ar: eh.scalar_tensor_tensor(
                    out=cat[:, hp, :], in0=ar[:, hp, :], scalar=gat[:, hp:hp + 1], in1=rr[:, :], op0=ALU.mult, op1=ALU.mult),
                    r=[("ar", q), "gat", "rr"], w=[("cata", q)])
            for d in range(8):
                g = d % 2
                for k in range(8):
                    sc.op("pe", lambda eh, g=g, d=d, k=k, cat=cat: eh.matmul(
                        ps_y[g][:, :], lhsT=womix[:, k * D + d * 128:k * D + d * 128 + 128], rhs=cat[:, k, :],
                        start=(k == 0), stop=(k == 7)), r=["womix", ("cata", q), ("catu", q)], w=[("ps_y", g)])
                sc.op("act", lambda eh, g=g, d=d: eh.activation(out=yT[:, d, :], in_=ps_y[g][:, :], func=AF.Copy),
                      r=[("ps_y", g)], w=[("yT", q)])

        def Tl(j):
            par = j % 2
            q = j % 2
            sL = selv[:, 2 * par:2 * par + 1]
            sH = selv[:, 2 * par + 1:2 * par + 2]
            h1L, h1H, t32, yT = h1L2[q], h1H2[q], t322[q], yT2[q]
            _norm_stats(sc, yT, ("yT", q), sq, ones, ps_ss, rs, "rs", 8, NT, 1.0 / D, RMS_EPS)
            sc.op("act", lambda eh, sL=sL, h1L=h1L: eh.activation(out=h1L[:, :, :], in_=h1L[:, :, :], func=AF.Copy, scale=sL),
                  r=[("h1L", q), "selv"], w=[("h1L", q)])
            sc.op("dve", lambda eh, sH=sH, h1L=h1L, h1H=h1H: eh.scalar_tensor_tensor(
                out=h1L[:, :, :], in0=h1H[:, :, :], scalar=sH, in1=h1L[:, :, :], op0=ALU.mult, op1=ALU.add),
                r=[("h1L", q), ("h1H", q), "selv"], w=[("h1L", q)])
            for k in range(8):
                sc.op("dve", lambda eh, k=k, t32=t32: eh.scalar_tensor_tensor(
                    out=t32[:, k, :], in0=yT[:, k, :], scalar=cB(k), in1=rs[:, :], op0=ALU.mult, op1=ALU.mult),
                    r=[("yT", q), "rs", "cf"], w=[("t32", q, k)])
                sc.op("pool", lambda eh, k=k, t32=t32, h1L=h1L: eh.tensor_tensor(
                    out=t32[:, k, :], in0=t32[:, k, :], in1=h1L[:, k, :], op=ALU.add),
                    r=[("t32", q, k), ("h1L", q)], w=[("t32", q, k)])
            sc.dma("h2", q, 2, [(T["h2"][:, :, NT * j:NT * j + NT], t32[:, :, :])], r=[("t32", q, k) for k in range(8)])

        load_in(0)
        load_in(1)
        H(0)
        for j in range(8):
            if j + 1 < 8:
                H(j + 1)
            Tl(j)
            if j + 2 < 8:
                load_in(j + 2)
        return sc.emit()


def build_program(stop_after=5, debug=False):
    nc = bass.Bass("TRN2", target_bir_lowering=False)
    din = lambda n, shp, dt=F32: nc.dram_tensor(n, shp, dt, kind="ExternalInput").ap()
    dsc = lambda n, shp, dt: nc.dram_tensor(n, shp, dt, kind="Internal").ap()
    dout = lambda n, shp, dt=F32: nc.dram_tensor(n, shp, dt, kind="ExternalOutput").ap()
    T = {}
    T["xT"] = din("xT", [128, 8, S])
    T["cT"] = din("cT", [128, 8])
    T["wada"] = din("wada", [72, 128, 1024])
    T["bada"] = din("bada", [128, 72])
    T["gvec"] = din("gvec", [128, 48])
    T["w1in"] = din("w1in", [128, 8 * 2 * DFF])
    T["w1out"] = din("w1out", [128, FC * D])
    T["w2in"] = din("w2in", [128, 8 * 2 * DFF])
    T["w2out"] = din("w2out", [128, FC * D])
    T["wmix"] = din("wmix", [128, 8 * 2560])
    T["womix"] = din("womix", [128, 8 * D])
    T["ident"] = din("ident", [128, 128])
    T["bd"] = din("bd", [128, 128])
    T["convw"] = din("convw", [128, 124])
    T["cvec"] = din("cvec", [128, 12])
    T["gattn"] = din("gattn", [128, 4])
    T["selv"] = din("selv", [128, 4])
    T["masks"] = din("masks", [128, 16, 512], BF16)
    T["outT"] = dout("outT", [128, 8, NOWN])
    T["h1"] = dsc("h1", [128, 8, S], F32)
    T["hm"] = dsc("hm", [128, 8, S + 32], BF16)
    T["KT"] = dsc("KT", [128, 4, S], BF16)
    T["V"] = dsc("V", [128, 4, 64, 128], BF16)
    T["QT"] = dsc("QT", [128, 4, NOWN], BF16)
    T["U2T"] = dsc("U2T", [128, 4, NOWN], BF16)
    T["AR"] = dsc("AR", [128, 4, NOWN], F32)
    T["h2"] = dsc("h2", [128, 8, NOWN], F32)
    T["w2in_bf"] = dsc("w2in_bf", [128, 8 * 2 * DFF], BF16)
    T["w2out_bf"] = dsc("w2out_bf", [128, FC * D], BF16)
    T["wmix_bf"] = dsc("wmix_bf", [128, 8 * 2560], BF16)
    T["womix_bf"] = dsc("womix_bf", [128, 8 * D], BF16)
    dbg = None
    if debug:
        dbg = {"modT": dout("dbg_modT", [128, 72])}
        dnames = {"h1": ([128, 8, S], F32), "hm": ([128, 8, S + 32], BF16), "KT": ([128, 4, S], BF16),
                  "V": ([128, 4, 64, 128], BF16), "QT": ([128, 4, NOWN], BF16), "U2T": ([128, 4, NOWN], BF16),
                  "AR": ([128, 4, NOWN], F32), "h2": ([128, 8, NOWN], F32)}
        for n, (shp, dt) in dnames.items():
            dbg[n] = dout("dbg_" + n, shp, dt)
    with nc.sbuf_tensor("cf", [128, 72], F32) as cf:
        phase_ffn(nc, T, cf, 1, dbg)
        if stop_after >= 2:
            phase_inproj(nc, T, cf, dbg)
        if stop_after >= 3:
            if debug:
                phase_attn(nc, T, dbg, hp_list=DBG_HP, slot_list=DBG_SLOTS)
            else:
                phase_attn(nc, T, dbg)
        if stop_after >= 4:
            phase_outproj(nc, T, cf, dbg)
        if stop_after >= 5:
            phase_ffn(nc, T, cf, 2, dbg)
        if debug:
            with ExitStack() as st:
                sc = Sched(nc, st, "DBG")
                for n in dnames:
                    full = tuple(slice(None) for _ in dnames[n][0])
                    sc.dma("d" + n, 0, 1, [(dbg[n][full], T[n][full])])
                sc.emit()
    return nc


WARM_PE = False
DBG_HP = (0, 1, 2, 3)
DBG_SLOTS = tuple(range(8))

def _fm(v, nchunk):
    return np.ascontiguousarray(np.asarray(v, dtype=np.float32).reshape(nchunk, 128).T)


def _wl(w, kc):
    w = np.asarray(w, dtype=np.float32)
    n = w.shape[1]
    return np.ascontiguousarray(w.reshape(kc, 128, n).transpose(1, 0, 2).reshape(128, kc * n))


def prep_shared(inp):
    sh = {}
    wa = np.asarray(inp["w_ada"], dtype=np.float32)
    sh["wada"] = np.ascontiguousarray(wa.reshape(8, 128, 72, 128).transpose(2, 1, 0, 3).reshape(72, 128, 1024))
    sh["bada"] = _fm(inp["b_ada"], 72)
    sh["gvec"] = np.ascontiguousarray(np.concatenate(
        [_fm(inp[n], 8) for n in ("g_pre_ff1", "g_post_ff1", "g_pre_mix", "g_post_mix", "g_pre_ff2", "g_post_ff2")],
        axis=1))
    sh["w1in"] = _wl(inp["ff1_w_in"], 8)
    sh["w1out"] = _wl(inp["ff1_w_out"], FC)
    sh["w2in"] = _wl(inp["ff2_w_in"], 8)
    sh["w2out"] = _wl(inp["ff2_w_out"], FC)
    return sh


def prep_core(inp, core):
    b, p = core // 2, core % 2
    x = np.asarray(inp["x"], dtype=np.float32)[b]
    xf = x[::-1, :]
    m = {}
    m["xT"] = np.ascontiguousarray(xf.T.reshape(8, 128, S).transpose(1, 0, 2))
    m["cT"] = _fm(np.asarray(inp["c"], dtype=np.float32)[b], 8)
    return m


def _own_off(p, j):
    return (j % 2) if p == 0 else 1 - (j % 2)


def prep_shared2(inp, sh):
    sh["wmix"] = _wl(inp["w_in_mix"], 8)
    sh["womix"] = _wl(inp["w_out_mix"], 8)
    sh["ident"] = np.eye(128, dtype=np.float32)
    bd = np.zeros((128, 128), np.float32)
    bd[:64, :64] = 1.0 / 64
    bd[64:, 64:] = 1.0 / 64
    sh["bd"] = bd
    cw = np.asarray(inp["conv_w"], dtype=np.float32)
    sh["convw"] = np.ascontiguousarray(cw.T.reshape(4, 128, 31).transpose(1, 0, 2).reshape(128, 124))
    sh["cvec"] = np.ascontiguousarray(np.concatenate(
        [_fm(inp["conv_b"], 4), _fm(inp["conv_ln_g"], 4), _fm(inp["conv_ln_b"], 4)], axis=1))
    sh["gattn"] = np.ascontiguousarray(np.asarray(inp["g_attn_out"], dtype=np.float32).reshape(4, 128).T)
    return sh


def prep_core2(m, core):
    p = core % 2
    selv = np.zeros((128, 4), np.float32)
    masks = np.zeros((128, 2, 2, 4, 512), np.float32)
    pp = np.arange(128)[:, None]
    jj = np.arange(512)[None, :]
    for par in range(2):
        o = _own_off(p, par)
        selv[:, 2 * par] = 1.0 if o == 0 else 0.0
        selv[:, 2 * par + 1] = 1.0 if o == 1 else 0.0
        for r in range(4):
            diag = np.where(jj <= r * 128 + pp, NEG, 0.0).astype(np.float32)
            if o == 0:
                masks[:, par, 0, r] = diag
            else:
                masks[:, par, 0, r] = NEG
                masks[:, par, 1, r] = diag
    m["selv"] = selv
    m["masks"] = np.ascontiguousarray(masks.reshape(128, 16, 512)).astype(ml_dtypes.bfloat16)
    return m


IN_NAMES = ["xT", "cT", "wada", "bada", "gvec", "w1in", "w1out", "w2in", "w2out", "wmix", "womix", "ident", "bd",
            "convw", "cvec", "gattn", "selv", "masks"]
_PROG = {}


def kernel(**inputs):
    sh = prep_shared2(inputs, prep_shared(inputs))
    in_maps = []
    for core in range(8):
        m = dict(sh)
        m.update(prep_core(inputs, core))
        prep_core2(m, core)
        in_maps.append({k: m[k] for k in IN_NAMES})
    if "nc" not in _PROG:
        _PROG["nc"] = build_program()
    res = run_bass_kernel_spmd(_PROG["nc"], in_maps, core_ids=list(range(8)))
    out = np.zeros((4, S, D), np.float32)
    for core in range(8):
        b, p = core // 2, core % 2
        oT = np.asarray(res.results[core]["outT"])
        for j in range(8):
            tile = 2 * j + _own_off(p, j)
            blk = oT[:, :, 512 * j:512 * j + 512].transpose(2, 1, 0).reshape(512, D)
            out[b, S - tile * 512 - 512:S - tile * 512, :] = blk[::-1]
    return out
```

```python
import os
import numpy as np
import ml_dtypes
from contextlib import ExitStack
import concourse.bass as bass
import concourse.mybir as mybir
from concourse.bass_utils import run_bass_kernel_spmd

F32 = mybir.dt.float32
BF16 = mybir.dt.bfloat16
AF = mybir.ActivationFunctionType
ALU = mybir.AluOpType

D = 1024
KC = 8
S = 8192
DFF = 2816
FC = 22
NOWN = 4096
TT = 256
RMS_EPS = 1e-6
LN_EPS = 1e-5
NEG = -30720.0
ENGS = ("pe", "act", "dve", "pool", "sp")
BLK = {"pe": "tensor", "act": "scalar", "dve": "vector", "pool": "gpsimd", "sp": "sync"}


class _Op:
    __slots__ = ("eng", "fn", "deps", "signal", "sig_idx", "is_dma", "dsem", "dkey", "dval")


class Sched:
    def __init__(self, nc, st, tag):
        self.nc, self.st, self.tag = nc, st, tag
        self.ops = []
        self.last_w = {}
        self.readers = {}
        self.streams = {}
        self.nd = 0

    def _sem(self, name):
        return self.st.enter_context(self.nc.semaphore(f"{self.tag}_{name}"))

    def _add(self, op, reads, writes):
        deps = op.deps
        for k in reads:
            w = self.last_w.get(k)
            if w is not None:
                deps.append((w, 0))
        for k in writes:
            w = self.last_w.get(k)
            if w is not None:
                deps.append((w, 1))
            rd = self.readers.get(k)
            if rd:
                for r in rd.values():
                    deps.append((r, 2))
        for k in writes:
            self.last_w[k] = op
            self.readers[k] = {}
        for k in reads:
            if op.is_dma:
                self.nd += 1
                rk = ("dma", self.nd)
            else:
                rk = op.eng
            self.readers.setdefault(k, {})[rk] = op
        self.ops.append(op)

    def op(self, eng, fn, r=(), w=()):
        o = _Op()
        o.eng, o.fn, o.deps, o.signal, o.sig_idx, o.is_dma = eng, fn, [], False, 0, False
        self._add(o, r, w)
        return o

    def dma(self, stream, slot, nslots, pairs, r=(), w=()):
        stt = self.streams.get(stream)
        if stt is None:
            stt = self.streams[stream] = {
                "sems": [self._sem(f"d{stream}{i}") for i in range(nslots)], "cnt": [0] * nslots}
        o = _Op()
        o.eng, o.deps, o.signal, o.sig_idx, o.is_dma = "sp", [], False, 0, True
        o.fn = lambda eh, pairs=pairs: [eh.dma_start(out=a, in_=b) for (a, b) in pairs]
        o.dsem = stt["sems"][slot]
        o.dkey = (stream, slot)
        stt["cnt"][slot] += 16 * len(pairs)
        o.dval = stt["cnt"][slot]
        self._add(o, r, w)
        return o

    @staticmethod
    def _needs(op, p, kind):
        if p.is_dma:
            return True
        if p.eng == op.eng and not op.is_dma:
            return kind == 0 and p.eng != "pe"
        return True

    def emit(self):
        for op in self.ops:
            for (p, kind) in op.deps:
                if not p.is_dma and self._needs(op, p, kind):
                    p.signal = True
        cnt = {e: 0 for e in ENGS}
        for op in self.ops:
            if not op.is_dma and op.signal:
                cnt[op.eng] += 1
                op.sig_idx = cnt[op.eng]
        esem = {e: self._sem("e" + e) for e in ENGS if cnt[e] > 0}
        per = {e: [op for op in self.ops if op.eng == e] for e in ENGS}
        needs = self._needs
        streams = self.streams
        with self.nc.Block() as block:
            for e in ENGS:
                if not per[e]:
                    continue

                def body(eh, e=e):
                    waited = {}
                    for op in per[e]:
                        need = {}
                        for (p, kind) in op.deps:
                            if not needs(op, p, kind):
                                continue
                            if p.is_dma:
                                key, sem, val = ("d",) + p.dkey, p.dsem, p.dval
                            else:
                                key, sem, val = ("e", p.eng), esem[p.eng], p.sig_idx
                            if key not in need or need[key][1] < val:
                                need[key] = (sem, val)
                        for key, (sem, val) in need.items():
                            if waited.get(key, 0) >= val:
                                continue
                            eh.wait_ge(sem, val)
                            waited[key] = val
                        ins = op.fn(eh)
                        if op.is_dma:
                            for i in ins:
                                i.then_inc(op.dsem, 16)
                        elif op.signal:
                            ins.then_inc(esem[e], 1)
                    if e == "sp":
                        for stt in streams.values():
                            for sem, c in zip(stt["sems"], stt["cnt"]):
                                if c > 0:
                                    eh.wait_ge(sem, c)

                getattr(block, BLK[e])(body)
        return cnt


def _cast(sc, i, dst, src, r, w):
    e = ("dve", "pool", "act")[i % 3]
    if e == "act":
        sc.op("act", lambda eh: eh.activation(out=dst, in_=src, func=AF.Copy), r=r, w=w)
    else:
        sc.op(e, lambda eh: eh.tensor_copy(out=dst, in_=src), r=r, w=w)


def _load_weight(sc, stg, ctr, dst, dst_key, src, total, piece=1024):
    n = (total + piece - 1) // piece
    for i in range(n):
        a, b = i * piece, min(total, (i + 1) * piece)
        sl = ctr[0] % len(stg)
        ctr[0] += 1
        sc.dma("stg", sl, len(stg), [(stg[sl][:, 0:b - a], src[:, a:b])], w=[("stg", sl)])
        _cast(sc, ctr[0], dst[:, a:b], stg[sl][:, 0:b - a], r=[("stg", sl)], w=[dst_key])


def _norm_p1(sc, src, src_key, sq, nchunk):
    sc.op("act", lambda eh: eh.activation(out=sq[:, 0:nchunk, :], in_=src[:, 0:nchunk, :], func=AF.Square),
          r=[src_key], w=["sq"])


def _norm_p2(sc, sq, ones, ps_ss, rs, rs_key, nchunk, ntok, inv_n):
    for k in range(nchunk):
        sc.op("pe", lambda eh, k=k: eh.matmul(ps_ss[:, 0:ntok], lhsT=ones[:, :], rhs=sq[:, k, :],
                                              start=(k == 0), stop=(k == nchunk - 1)),
              r=["sq", "ones"], w=["ps_ss"])
    sc.op("act", lambda eh: eh.activation(out=rs[:, :], in_=ps_ss[:, 0:ntok], func=AF.Sqrt, bias=eps_ap(sc), scale=inv_n),
          r=["ps_ss", "eps"], w=[rs_key])
    sc.op("dve", lambda eh: eh.reciprocal(out=rs[:, :], in_=rs[:, :]), r=[rs_key], w=[rs_key])


def _norm_stats(sc, src, src_key, sq, ones, ps_ss, rs, rs_key, nchunk, ntok, inv_n, eps):
    _norm_p1(sc, src, src_key, sq, nchunk)
    _norm_p2(sc, sq, ones, ps_ss, rs, rs_key, nchunk, ntok, inv_n)


_EPS = {}


def eps_ap(sc):
    return _EPS["rms"]


def phase_ffn(nc, T, cf, which, dbg=None):
    tag = f"F{which}"
    ntiles = (S if which == 1 else NOWN) // TT
    s_idx = 0 if which == 1 else 2
    src_dram = T["xT"] if which == 1 else T["h2"]
    with ExitStack() as st:
        sc = Sched(nc, st, tag)
        sb = lambda n, shp, dt: st.enter_context(nc.sbuf_tensor(f"{tag}_{n}", shp, dt))
        pp = lambda n, shp, dt=F32: st.enter_context(nc.psum_tensor(f"{tag}_p_{n}", shp, dt))
        w_in = sb("w_in", [128, 8 * 2 * DFF], BF16)
        w_out = sb("w_out", [128, FC * D], BF16)
        stg = [sb(f"stg{i}", [128, 1024], F32) for i in range(2)]
        ones = sb("ones", [128, 128], BF16)
        epsb = sb("epsb", [128, 1], F32)
        xT = [sb(f"xT{i}", [128, 8, TT], F32) for i in range(2)]
        sq = sb("sq", [128, 8, TT], BF16)
        t32 = sb("t32", [128, 8, TT], F32)
        hT = [sb(f"hT{i}", [128, 8, TT], BF16) for i in range(2)]
        sg = [sb(f"sg{i}", [128, TT], F32) for i in range(2)]
        actT = sb("actT", [128, FC, TT], BF16)
        yT = sb("yT", [128, 8, TT], F32)
        hm = sb("hm", [128, 8, TT], BF16) if which == 1 else None
        rs1 = [sb(f"rs1{i}", [128, TT], F32) for i in range(2)]
        rs2 = sb("rs2", [128, TT], F32)
        rs3 = rs2
        ps_gu = [pp(f"gu{i}", [128, 512]) for i in range(2)]
        ps_y = [pp(f"y{i}", [128, 512]) for i in range(2)]
        ps_ss = pp("ss", [128, 512])
        ps_md = pp("md", [128, 512])
        _EPS["rms"] = epsb[:, 0:1]

        sc.op("pool", lambda eh: eh.memset(ones[:, :], 1.0), w=["ones"])
        sc.op("pool", lambda eh: eh.memset(epsb[:, :], RMS_EPS), w=["eps"])
        ctr = [0]

        if which == 1:
            scT = sb("scT", [128, 8], F32)
            gv = sb("gv", [128, 48], F32)
            bada = sb("bada", [128, 72], F32)
            modT = sb("modT", [128, 72], F32)
            zpad = sb("zpad", [128, 8, 32], BF16)
            sc.dma("c", 0, 1, [(scT[:, :], T["cT"][:, :]), (gv[:, :], T["gvec"][:, :]),
                               (bada[:, :], T["bada"][:, :])], w=["scT", "gv", "bada"])
            sc.op("act", lambda eh: eh.activation(out=scT[:, :], in_=scT[:, :], func=AF.Silu), r=["scT"], w=["scT"])
            for j in range(72):
                sl = ctr[0] % 2
                ctr[0] += 1
                sc.dma("stg", sl, 2, [(stg[sl][:, :], T["wada"][j, :, :])], w=[("stg", sl)])
                for k in range(8):
                    sc.op("pe", lambda eh, sl=sl, j=j, k=k: eh.matmul(
                        ps_md[:, j:j + 1], lhsT=stg[sl][:, k * 128:(k + 1) * 128], rhs=scT[:, k:k + 1],
                        start=(k == 0), stop=(k == 7)), r=[("stg", sl), "scT"], w=["ps_md"])
            sc.op("dve", lambda eh: eh.tensor_tensor(out=modT[:, :], in0=ps_md[:, 0:72], in1=bada[:, :], op=ALU.add),
                  r=["ps_md", "bada"], w=["modT"])
            for s in range(3):
                resw = 1.0 if s == 1 else 0.5
                A = cf[:, (3 * s) * 8:(3 * s) * 8 + 8]
                SH = cf[:, (3 * s + 1) * 8:(3 * s + 1) * 8 + 8]
                B = cf[:, (3 * s + 2) * 8:(3 * s + 2) * 8 + 8]
                m_sh = modT[:, (3 * s) * 8:(3 * s) * 8 + 8]
                m_sc = modT[:, (3 * s + 1) * 8:(3 * s + 1) * 8 + 8]
                m_g = modT[:, (3 * s + 2) * 8:(3 * s + 2) * 8 + 8]
                gpre = gv[:, (2 * s) * 8:(2 * s) * 8 + 8]
                gpost = gv[:, (2 * s + 1) * 8:(2 * s + 1) * 8 + 8]
                sc.op("dve", lambda eh, A=A, m_sc=m_sc, gpre=gpre: eh.scalar_tensor_tensor(
                    out=A, in0=m_sc, scalar=1.0, in1=gpre, op0=ALU.add, op1=ALU.mult), r=["modT", "gv"], w=["cf"])
                sc.op("dve", lambda eh, SH=SH, m_sh=m_sh: eh.tensor_copy(out=SH, in_=m_sh), r=["modT"], w=["cf"])
                sc.op("dve", lambda eh, B=B, m_g=m_g, gpost=gpost: eh.scalar_tensor_tensor(
                    out=B, in0=m_g, scalar=1.0, in1=gpost, op0=ALU.add, op1=ALU.mult), r=["modT", "gv"], w=["cf"])
                sc.op("dve", lambda eh, B=B, resw=resw: eh.tensor_scalar(
                    out=B, in0=B, scalar1=resw, scalar2=None, op0=ALU.mult), r=["cf"], w=["cf"])
            sc.op("pool", lambda eh: eh.memset(zpad[:, :, :], 0.0), w=["zpad"])
            sc.dma("zp", 0, 1, [(T["hm"][:, :, S:S + 32], zpad[:, :, :])], r=["zpad"])
            if dbg is not None:
                sc.dma("dbg", 0, 1, [(dbg["modT"][:, :], modT[:, :])], r=["modT"])

        if which == 1:
            _load_weight(sc, stg, ctr, w_in, "w_in", T["w1in"], 8 * 2 * DFF)
            _load_weight(sc, stg, ctr, w_out, "w_out", T["w1out"], FC * D)
            stgb = [sb(f"stgb{i}", [128, 512], BF16) for i in range(2)]
            jobs = []
            for (src, dst, tot) in ((T["w2in"], T["w2in_bf"], 8 * 2 * DFF), (T["w2out"], T["w2out_bf"], FC * D),
                                    (T["wmix"], T["wmix_bf"], 8 * 2560), (T["womix"], T["womix_bf"], 8 * D)):
                for a0 in range(0, tot, 512):
                    jobs.append((src[:, a0:a0 + 512], dst[:, a0:a0 + 512]))
            jobs.reverse()

            pc_state = {"n": 0, "load": None, "cast": None}

            def precast_job():
                if pc_state["cast"] is not None:
                    dst, sl = pc_state["cast"]
                    sc.dma("pco", sl, 2, [(dst, stgb[sl][:, :])], r=[("stgb", sl)])
                    pc_state["cast"] = None
                if pc_state["load"] is not None:
                    dst, sl = pc_state["load"]
                    sc.op("act", lambda eh, sl=sl: eh.activation(out=stgb[sl][:, :], in_=stg[sl][:, 0:512], func=AF.Copy),
                          r=[("stg", sl)], w=[("stgb", sl)])
                    pc_state["cast"] = (dst, sl)
                    pc_state["load"] = None
                if jobs:
                    src, dst = jobs.pop()
                    sl = pc_state["n"] % 2
                    pc_state["n"] += 1
                    sc.dma("stg", sl, 2, [(stg[sl][:, 0:512], src)], w=[("stg", sl)])
                    pc_state["load"] = (dst, sl)

            def precast_pending():
                return bool(jobs) or pc_state["load"] is not None or pc_state["cast"] is not None
        else:
            for i_ in range(4):
                a0, b0 = i_ * 11264, (i_ + 1) * 11264
                sc.dma("wld", i_, 4, [(w_in[:, a0:b0], T["w2in_bf"][:, a0:b0])], w=["w_in"])
            for i_ in range(2):
                a0, b0 = i_ * 11264, (i_ + 1) * 11264
                sc.dma("wld2", i_, 2, [(w_out[:, a0:b0], T["w2out_bf"][:, a0:b0])], w=["w_out"])

            def precast_job():
                return

        cA = lambda s, k: cf[:, (3 * s) * 8 + k:(3 * s) * 8 + k + 1]
        cS = lambda s, k: cf[:, (3 * s + 1) * 8 + k:(3 * s + 1) * 8 + k + 1]
        cB = lambda s, k: cf[:, (3 * s + 2) * 8 + k:(3 * s + 2) * 8 + k + 1]

        YT = [("yT", k) for k in range(8)]

        def mod_pair(src, src_keys, rs, rs_key, s, dst, dst_key, k):
            sc.op("dve", lambda eh: eh.scalar_tensor_tensor(
                out=t32[:, k, :], in0=src[:, k, :], scalar=cA(s, k), in1=rs[:, :], op0=ALU.mult, op1=ALU.mult),
                r=[src_keys[k], rs_key, "cf"], w=[("t32", k)])
            sc.op("act", lambda eh: eh.activation(
                out=dst[:, k, :], in_=t32[:, k, :], func=AF.Identity, bias=cS(s, k), scale=1.0),
                r=[("t32", k), "cf"], w=[dst_key])

        def load_x(i):
            b = i % 2
            sc.dma("x", b, 2, [(xT[b][:, :, :], src_dram[:, :, i * TT:(i + 1) * TT])], w=[("xT", b)])

        def stage_P1(i):
            b = i % 2
            _norm_p1(sc, xT[b], ("xT", b), sq, 8)

        def stage_P2(i):
            b = i % 2
            _norm_p2(sc, sq, ones, ps_ss, rs1[b], ("rs1", b), 8, TT, 1.0 / D)

        def stage_P3(i, ks):
            b = i % 2
            for k in ks:
                mod_pair(xT[b], [("xT", b)] * 8, rs1[b], ("rs1", b), s_idx, hT[b], ("hT", b), k)

        def stage_G(i, hooks):
            b = i % 2
            for f in range(FC):
                g = f % 2
                for half in range(2):
                    col = half * DFF + f * 128
                    for k in range(8):
                        sc.op("pe", lambda eh, g=g, half=half, col=col, k=k: eh.matmul(
                            ps_gu[g][:, half * TT:(half + 1) * TT],
                            lhsT=w_in[:, k * 2 * DFF + col:k * 2 * DFF + col + 128], rhs=hT[b][:, k, :],
                            start=(k == 0), stop=(k == 7)), r=["w_in", ("hT", b)], w=[("ps_gu", g)])
                sc.op("act", lambda eh, g=g: eh.activation(out=sg[g][:, :], in_=ps_gu[g][:, 0:TT], func=AF.Silu),
                      r=[("ps_gu", g)], w=[("sg", g)])
                sc.op("dve", lambda eh, g=g, f=f: eh.tensor_tensor(
                    out=actT[:, f, :], in0=sg[g][:, :], in1=ps_gu[g][:, TT:2 * TT], op=ALU.mult),
                    r=[("sg", g), ("ps_gu", g)], w=["actT"])
                for h in hooks.get(f, ()):
                    h()

        def stage_O1(i):
            for d in range(8):
                g = d % 2
                for f in range(FC):
                    sc.op("pe", lambda eh, g=g, d=d, f=f: eh.matmul(
                        ps_y[g][:, 0:TT], lhsT=w_out[:, f * D + d * 128:f * D + d * 128 + 128], rhs=actT[:, f, :],
                        start=(f == 0), stop=(f == FC - 1)), r=["w_out", "actT"], w=[("ps_y", g)])
                sc.op("act", lambda eh, g=g, d=d: eh.activation(out=yT[:, d, :], in_=ps_y[g][:, 0:TT], func=AF.Copy),
                      r=[("ps_y", g)], w=[("yT", d)])

        def norm_p1_yT():
            sc.op("act", lambda eh: eh.activation(out=sq[:, :, :], in_=yT[:, :, :], func=AF.Square), r=YT, w=["sq"])

        def stage_O2a1(i):
            norm_p1_yT()

        def stage_O2a2(i):
            _norm_p2(sc, sq, ones, ps_ss, rs2, "rs2", 8, TT, 1.0 / D)

        def stage_O2a3(i, ks):
            b = i % 2
            for k in ks:
                sc.op("dve", lambda eh, k=k: eh.scalar_tensor_tensor(
                    out=t32[:, k, :], in0=yT[:, k, :], scalar=cB(s_idx, k), in1=rs2[:, :], op0=ALU.mult, op1=ALU.mult),
                    r=[("yT", k), "rs2", "cf"], w=[("t32", k)])
                sc.op("pool", lambda eh, k=k: eh.tensor_tensor(
                    out=yT[:, k, :], in0=t32[:, k, :], in1=xT[b][:, k, :], op=ALU.add),
                    r=[("t32", k), ("xT", b)], w=[("yT", k)])
            if 7 in ks:
                if which == 1:
                    sc.dma("h1", 0, 1, [(T["h1"][:, :, i * TT:(i + 1) * TT], yT[:, :, :])], r=YT)
                else:
                    sc.dma("out", 0, 1, [(T["outT"][:, :, i * TT:(i + 1) * TT], yT[:, :, :])], r=YT)

        def stage_O2b1(i):
            if which == 1:
                norm_p1_yT()

        def stage_O2b2(i):
            if which == 1:
                _norm_p2(sc, sq, ones, ps_ss, rs3, "rs2", 8, TT, 1.0 / D)

        def stage_O2b3(i, ks):
            if which != 1:
                return
            for k in ks:
                mod_pair(yT, YT, rs3, "rs2", 1, hm, "hm", k)
            if 7 in ks:
                sc.dma("hm", 0, 1, [(T["hm"][:, :, i * TT:(i + 1) * TT], hm[:, :, :])], r=["hm"])

        PAIRS = [(0, 1), (2, 3), (4, 5), (6, 7)]

        def tile_hooks(i):
            hooks = {}
            add = lambda f, fn: hooks.setdefault(f, []).append(fn)
            if i >= 1:
                add(0, lambda: stage_O2a1(i - 1))
                add(2, lambda: stage_O2a2(i - 1))
                for n_, ks in enumerate(PAIRS):
                    add(3 + n_, lambda ks=ks: stage_O2a3(i - 1, ks))
                add(7, lambda: stage_O2b1(i - 1))
                add(9, lambda: stage_O2b2(i - 1))
                for n_, ks in enumerate(PAIRS):
                    add(10 + n_, lambda ks=ks: stage_O2b3(i - 1, ks))
                if i + 1 < ntiles:
                    add(7, lambda: load_x(i + 1))
            for f_ in (1, 5, 8, 11, 13, 19):
                add(f_, precast_job)
            if i + 1 < ntiles:
                add(14, lambda: stage_P1(i + 1))
                add(16, lambda: stage_P2(i + 1))
                for n_, ks in enumerate(PAIRS):
                    add(17 + n_, lambda ks=ks: stage_P3(i + 1, ks))
            return hooks

        load_x(0)
        load_x(1)
        stage_P1(0)
        stage_P2(0)
        stage_P3(0, range(8))
        for w_ in range(48):
            sc.op("pe", lambda eh: eh.matmul(ps_md[:, :], lhsT=w_in[:, 512:640], rhs=w_in[:, 0:512], start=True, stop=True),
                  r=["w_in"], w=["ps_md"])
        for i in range(ntiles):
            stage_G(i, tile_hooks(i))
            stage_O1(i)
        while which == 1 and precast_pending():
            precast_job()
        last = ntiles - 1
        stage_O2a1(last)
        stage_O2a2(last)
        stage_O2a3(last, range(8))
        stage_O2b1(last)
        stage_O2b2(last)
        stage_O2b3(last, range(8))
        cnt = sc.emit()
        return cnt


def phase_inproj(nc, T, cf, dbg=None):
    tag = "B"
    NT = 512
    with ExitStack() as st:
        sc = Sched(nc, st, tag)
        sb = lambda n, shp, dt: st.enter_context(nc.sbuf_tensor(f"{tag}_{n}", shp, dt))
        pp = lambda n, shp, dt=F32: st.enter_context(nc.psum_tensor(f"{tag}_p_{n}", shp, dt))
        wmix = sb("wmix", [128, 8 * 2560], BF16)
        idf = sb("idf", [128, 128], F32)
        idb = sb("idb", [128, 128], BF16)
        ones32 = sb("ones32", [128, 128], F32)
        diag = sb("diag", [128, 124, 128], BF16)
        cw = sb("cw", [128, 124], F32)
        cvec = sb("cvec", [128, 12], F32)
        selv = sb("selv", [128, 4], F32)
        epsb = sb("epsb", [128, 1], F32)
        hmb = [sb(f"hmb{i}", [128, 8, NT], BF16) for i in range(2)]
        kt_sb = [sb(f"kt{i}", [128, 4, NT], BF16) for i in range(2)]
        v_sb = [sb(f"v{i}", [128, 4, 512], BF16) for i in range(2)]
        hmx = sb("hmx", [128, 8, 1056], BF16)
        htmp = sb("htmp", [128, 8, 544], BF16)
        hown = sb("hown", [128, 8, 544], BF16)
        qt_sb = sb("qt", [128, 4, NT], BF16)
        sgm = [sb(f"sgm{i}", [128, 544], F32) for i in range(2)]
        ub = [sb(f"u{i}", [128, 4, 544], BF16) for i in range(2)]
        cT = sb("cT", [128, 4, NT], F32)
        csq = sb("csq", [128, 4, NT], F32)
        mean_sb = sb("mean", [128, NT], F32)
        var_sb = sb("var", [128, NT], F32)
        tln = sb("tln", [128, 4, NT], F32)
        u2 = sb("u2", [128, 4, NT], BF16)
        ps_a = [pp(f"a{i}", [128, 512]) for i in range(3)]
        ps_h = [pp(f"h{i}", [128, 512]) for i in range(2)]
        ps_m = pp("m", [128, 512])
        ps_q = pp("q", [128, 512])

        sc.op("pool", lambda eh: eh.memset(ones32[:, :], 1.0 / 512.0), w=["ones32"])
        sc.op("pool", lambda eh: eh.memset(epsb[:, :], LN_EPS), w=["eps"])
        sc.dma("c", 0, 1, [(idf[:, :], T["ident"][:, :]), (cw[:, :], T["convw"][:, :]),
                           (cvec[:, :], T["cvec"][:, :]), (selv[:, :], T["selv"][:, :])],
               w=["idf", "cw", "cvec", "selv"])
        sc.op("dve", lambda eh: eh.tensor_copy(out=idb[:, :], in_=idf[:, :]), r=["idf"], w=["idb"])
        for i in range(124):
            if i % 2 == 0:
                sc.op("dve", lambda eh, i=i: eh.tensor_scalar(out=diag[:, i, :], in0=idf[:, :], scalar1=cw[:, i:i + 1],
                                                              scalar2=None, op0=ALU.mult), r=["idf", "cw"], w=["diag"])
            else:
                sc.op("act", lambda eh, i=i: eh.activation(out=diag[:, i, :], in_=idf[:, :], func=AF.Copy,
                                                           scale=cw[:, i:i + 1]), r=["idf", "cw"], w=["diag"])
        for i_ in range(4):
            a0, b0 = i_ * 5120, (i_ + 1) * 5120
            sc.dma("wld", i_, 4, [(wmix[:, a0:b0], T["wmix_bf"][:, a0:b0])], w=["wmix"])
        WM = lambda k, c0: wmix[:, k * 2560 + c0:k * 2560 + c0 + 128]
        ev = [0]

        def evac(dst, src, r, w, scale=None):
            ev[0] += 1
            if ev[0] % 2 == 0:
                sc.op("act", lambda eh: eh.activation(out=dst, in_=src, func=AF.Copy,
                                                      scale=(1.0 if scale is None else scale)), r=r, w=w)
            elif scale is None:
                sc.op("dve", lambda eh: eh.tensor_copy(out=dst, in_=src), r=r, w=w)
            else:
                sc.op("dve", lambda eh: eh.tensor_scalar(out=dst, in0=src, scalar1=scale, scalar2=None, op0=ALU.mult),
                      r=r, w=w)

        def load_hm(i):
            b = i % 2
            sc.dma("hm", b, 2, [(hmb[b][:, :, :], T["hm"][:, :, i * NT:(i + 1) * NT])], w=[("hmb", b)])

        load_hm(0)
        pa = [0]
        for i in range(16):
            b = i % 2
            if i + 1 < 16:
                load_hm(i + 1)
            for hp in range(4):
                a = pa[0] % 3
                pa[0] += 1
                for k in range(8):
                    sc.op("pe", lambda eh, a=a, hp=hp, k=k, b=b: eh.matmul(
                        ps_a[a][:, :], lhsT=WM(k, 512 + hp * 128), rhs=hmb[b][:, k, :], start=(k == 0), stop=(k == 7)),
                        r=["wmix", ("hmb", b)], w=[("ps_a", a)])
                evac(kt_sb[b][:, hp, :], ps_a[a][:, :], r=[("ps_a", a)], w=[("kt", b)])
            sc.dma("kt", b, 2, [(T["KT"][:, :, i * NT:(i + 1) * NT], kt_sb[b][:, :, :])], r=[("kt", b)])
            for tb in range(4):
                a = pa[0] % 3
                pa[0] += 1
                for k in range(8):
                    sc.op("pe", lambda eh, a=a, tb=tb, k=k, b=b: eh.matmul(
                        ps_a[a][:, :], lhsT=hmb[b][:, k, tb * 128:(tb + 1) * 128],
                        rhs=wmix[:, k * 2560 + 1024:k * 2560 + 1536], start=(k == 0), stop=(k == 7)),
                        r=["wmix", ("hmb", b)], w=[("ps_a", a)])
                evac(v_sb[b][:, tb, :], ps_a[a][:, :], r=[("ps_a", a)], w=[("v", b)])
            sc.dma("v", b, 2, [(T["V"][:, hp, 4 * i:4 * i + 4, :], v_sb[b][:, :, hp * 128:(hp + 1) * 128])
                               for hp in range(4)], r=[("v", b)])

        def X(j):
            u = ub[j % 2]
            uk = ("u", j % 2)
            par = j % 2
            sL = selv[:, 2 * par:2 * par + 1]
            sH = selv[:, 2 * par + 1:2 * par + 2]
            sc.dma("hmx", 0, 1, [(hmx[:, :, :], T["hm"][:, :, 1024 * j:1024 * j + 1056])], w=["hmx"])
            sc.op("act", lambda eh, sL=sL: eh.activation(out=htmp[:, :, :], in_=hmx[:, :, 0:544], func=AF.Copy, scale=sL),
                  r=["hmx", "selv"], w=["htmp"])
            sc.op("dve", lambda eh, sH=sH: eh.scalar_tensor_tensor(
                out=hown[:, :, :], in0=hmx[:, :, 512:1056], scalar=sH, in1=htmp[:, :, :], op0=ALU.mult, op1=ALU.add),
                r=["hmx", "htmp", "selv"], w=["hown"])
            for hp in range(4):
                for k in range(8):
                    sc.op("pe", lambda eh, hp=hp, k=k: eh.matmul(
                        ps_q[:, :], lhsT=WM(k, hp * 128), rhs=hown[:, k, 0:512], start=(k == 0), stop=(k == 7)),
                        r=["wmix", "hown"], w=["ps_q"])
                evac(qt_sb[:, hp, :], ps_q[:, :], r=["ps_q"], w=["qt"], scale=0.125)
            sc.dma("qt", 0, 1, [(T["QT"][:, :, 512 * j:512 * j + 512], qt_sb[:, :, :])], r=["qt"])
            for cc in range(4):
                g = cc % 2
                for (c0, n, off) in ((0, 512, 0), (512, 32, 0)):
                    pv = ps_a[0] if n == 512 else ps_a[2]
                    pg = ps_a[1] if n == 512 else ps_a[2]
                    o2 = 0 if n == 512 else 64
                    for k in range(8):
                        sc.op("pe", lambda eh, pv=pv, cc=cc, k=k, c0=c0, n=n: eh.matmul(
                            pv[:, 0:n], lhsT=WM(k, 1536 + cc * 128), rhs=hown[:, k, c0:c0 + n],
                            start=(k == 0), stop=(k == 7)), r=["wmix", "hown"], w=[("ps_a", 0 if n == 512 else 2)])
                    for k in range(8):
                        sc.op("pe", lambda eh, pg=pg, cc=cc, k=k, c0=c0, n=n, o2=o2: eh.matmul(
                            pg[:, o2:o2 + n], lhsT=WM(k, 2048 + cc * 128), rhs=hown[:, k, c0:c0 + n],
                            start=(k == 0), stop=(k == 7)), r=["wmix", "hown"], w=[("ps_a", 1 if n == 512 else 2)])
                    sc.op("act", lambda eh, pg=pg, g=g, c0=c0, n=n, o2=o2: eh.activation(
                        out=sgm[g][:, c0:c0 + n], in_=pg[:, o2:o2 + n], func=AF.Sigmoid),
                        r=[("ps_a", 1 if n == 512 else 2)], w=[("sgm", g)])
                    sc.op("dve", lambda eh, pv=pv, g=g, cc=cc, c0=c0, n=n: eh.tensor_tensor(
                        out=u[:, cc, c0:c0 + n], in0=sgm[g][:, c0:c0 + n], in1=pv[:, 0:n], op=ALU.mult),
                        r=[("sgm", g), ("ps_a", 0 if n == 512 else 2)], w=[uk])

        def Y(j):
            u = ub[j % 2]
            uk = ("u", j % 2)
            for cc in range(4):
                h = cc % 2
                for k in range(31):
                    sc.op("pe", lambda eh, h=h, cc=cc, k=k: eh.matmul(
                        ps_h[h][:, :], lhsT=diag[:, cc * 31 + k, :], rhs=u[:, cc, 30 - k:30 - k + 512],
                        start=(k == 0), stop=(k == 30)), r=["diag", uk], w=[("ps_h", h)])
                sc.op("act", lambda eh, h=h, cc=cc: eh.activation(
                    out=cT[:, cc, :], in_=ps_h[h][:, :], func=AF.Identity, bias=cvec[:, cc:cc + 1], scale=1.0),
                    r=[("ps_h", h), "cvec"], w=["cT"])
            sc.op("act", lambda eh: eh.activation(out=csq[:, :, :], in_=cT[:, :, :], func=AF.Square), r=["cT"], w=["csq"])
            for cc in range(4):
                sc.op("pe", lambda eh, cc=cc: eh.matmul(ps_m[:, :], lhsT=ones32[:, :], rhs=cT[:, cc, :],
                                                        start=(cc == 0), stop=(cc == 3)), r=["ones32", "cT"], w=["ps_m"])
            sc.op("act", lambda eh: eh.activation(out=mean_sb[:, :], in_=ps_m[:, :], func=AF.Copy), r=["ps_m"], w=["mean"])
            for cc in range(4):
                sc.op("pe", lambda eh, cc=cc: eh.matmul(ps_q[:, :], lhsT=ones32[:, :], rhs=csq[:, cc, :],
                                                        start=(cc == 0), stop=(cc == 3)), r=["ones32", "csq"], w=["ps_q"])
            sc.op("dve", lambda eh: eh.tensor_tensor(out=var_sb[:, :], in0=mean_sb[:, :], in1=mean_sb[:, :], op=ALU.mult),
                  r=["mean"], w=["var"])
            sc.op("dve", lambda eh: eh.tensor_tensor(out=var_sb[:, :], in0=ps_q[:, :], in1=var_sb[:, :], op=ALU.subtract),
                  r=["ps_q", "var"], w=["var"])
            sc.op("act", lambda eh: eh.activation(out=var_sb[:, :], in_=var_sb[:, :], func=AF.Sqrt, bias=epsb[:, 0:1], scale=1.0),
                  r=["var", "eps"], w=["var"])
            sc.op("dve", lambda eh: eh.reciprocal(out=var_sb[:, :], in_=var_sb[:, :]), r=["var"], w=["var"])
            for cc in range(4):
                sc.op("pool", lambda eh, cc=cc: eh.tensor_tensor(out=tln[:, cc, :], in0=cT[:, cc, :], in1=mean_sb[:, :],
                                                                 op=ALU.subtract), r=["cT", "mean"], w=["tln"])
                sc.op("dve", lambda eh, cc=cc: eh.tensor_tensor(out=tln[:, cc, :], in0=tln[:, cc, :], in1=var_sb[:, :],
                                                                op=ALU.mult), r=["tln", "var"], w=["tln"])
                sc.op("act", lambda eh, cc=cc: eh.activation(
                    out=u2[:, cc, :], in_=tln[:, cc, :], func=AF.Silu, bias=cvec[:, 8 + cc:9 + cc], scale=cvec[:, 4 + cc:5 + cc]),
                    r=["tln", "cvec"], w=["u2"])
            sc.dma("u2", 0, 1, [(T["U2T"][:, :, 512 * j:512 * j + 512], u2[:, :, :])], r=["u2"])

        X(0)
        for j in range(8):
            if j + 1 < 8:
                X(j + 1)
            Y(j)
        return sc.emit()


def phase_attn(nc, T, dbg=None, hp_list=(0, 1, 2, 3), slot_list=tuple(range(8))):
    tag = "C"
    with ExitStack() as st:
        sc = Sched(nc, st, tag)
        sb = lambda n, shp, dt: st.enter_context(nc.sbuf_tensor(f"{tag}_{n}", shp, dt))
        pp = lambda n, shp, dt=F32: st.enter_context(nc.psum_tensor(f"{tag}_p_{n}", shp, dt))
        idf = sb("idf", [128, 128], F32)
        idb = sb("idb", [128, 128], BF16)
        QT = sb("QT", [128, 4, NOWN], BF16)
        msk = sb("msk", [128, 16, 512], BF16)
        KT = [sb(f"KT{i}", [128, S], BF16) for i in range(2)]
        V = [sb(f"V{i}", [128, 64, 128], BF16) for i in range(2)]
        NB = 3
        ring = [sb(f"ring{i}", [128, 1 + 16 * 512], F32) for i in range(2)]
        dummy = sb("dummy", [128, 512], F32)
        wb = [[sb(f"w{i}_{e}", [128, 512], BF16) for e in range(2)] for i in range(NB)]
        wT = [[sb(f"wT{i}_{e}", [128, 512], BF16) for e in range(2)] for i in range(2)]
        o_sb = [sb(f"o{i}", [128, 512], F32) for i in range(2)]
        ps_z = [[pp(f"z{i}_{e}", [128, 512]) for e in range(2)] for i in range(2)]
        ps_t = [pp(f"t{e}", [128, 512]) for e in range(2)]
        ps_o = [pp(f"o{i}", [128, 512]) for i in range(2)]

        sc.dma("c", 0, 1, [(idf[:, :], T["ident"][:, :]), (msk[:, :, :], T["masks"][:, :, :]),
                           (QT[:, :, :], T["QT"][:, :, :])], w=["idf", "msk", "QT"])
        sc.op("dve", lambda eh: eh.tensor_copy(out=idb[:, :], in_=idf[:, :]), r=["idf"], w=["idb"])
        for e_ in range(2):
            sc.op("pool", lambda eh, e_=e_: eh.memset(ring[e_][:, 0:1], 1.0), w=[("I0", e_)])
        sc.op("pool", lambda eh: eh.memset(dummy[:, :], 1.0), w=["dummy"])

        def load_kv(hp, b=0):
            sc.dma("kv", b, 2, [(KT[b][:, :], T["KT"][:, hp, :]), (V[b][:, :, :], T["V"][:, hp, :, :])],
                   w=[("KT", b), ("V", b)])

        steps = []
        qctr = 0
        for hi, hp in enumerate(hp_list):
            for j in slot_list:
                for r in range(4):
                    nch = 16 - 2 * j
                    for c in range(nch):
                        steps.append(dict(hp=hp, hi=hi, j=j, r=r, c=c, nch=nch, q=qctr,
                                          first_hp=(j == slot_list[0] and r == 0 and c == 0)))
                    qctr += 1
        N = len(steps)
        for n, it in enumerate(steps):
            it["n"] = n

        def geom(r, c):
            k0 = 128 * r if c == 0 else 0
            W = 512 - k0
            lo = 0 if c == 0 else (512 - 128 * r) + 512 * (c - 1)
            return k0, W, lo

        def S0(it):
            n, hp, j, r, c = it["n"], it["hp"], it["j"], it["r"], it["c"]
            kb = it["hi"] % 2
            zb = n % 2
            kt = 2 * j + c
            q0 = 512 * j + 128 * r
            k0, W, lo = geom(r, c)
            masked = c < 2
            for e in range(2):
                sc.op("pe", lambda eh, e=e: eh.matmul(
                    ps_z[zb][e][:, 0:W], lhsT=QT[e * 64:(e + 1) * 64, hp, q0:q0 + 128],
                    rhs=KT[kb][e * 64:(e + 1) * 64, kt * 512 + k0:(kt + 1) * 512], start=True, stop=(not masked)),
                    r=["QT", ("KT", kb)], w=[("ps_z", zb, e)])
            if masked:
                mi = ((j % 2) * 2 + c) * 4 + r
                for e in range(2):
                    sc.op("pe", lambda eh, e=e: eh.matmul(ps_z[zb][e][:, 0:W], lhsT=idb[:, :], rhs=msk[:, mi, k0:512],
                                                          start=False, stop=True),
                          r=["idb", "msk"], w=[("ps_z", zb, e)])
            for e in range(2):
                sc.op("act", lambda eh, e=e: eh.activation(out=ps_z[zb][e][:, 0:W], in_=ps_z[zb][e][:, 0:W], func=AF.Sigmoid,
                                                           scale=-1.0), r=[("ps_z", zb, e)], w=[("ps_z", zb, e)])

        def S1(it):
            n, r, c = it["n"], it["r"], it["c"]
            b = n % NB
            zb = n % 2
            k0, W, lo = geom(r, c)
            for e in range(2):
                rg = ring[e]
                prev = [("I", e, c - 1)] if c > 0 else [("I0", e)]
                init = 1.0 if c == 0 else rg[:, lo:lo + 1]
                sc.op("dve", lambda eh, e=e, rg=rg, init=init: eh.tensor_tensor_scan(
                    out=rg[:, lo + 1:lo + 1 + W], data0=ps_z[zb][e][:, 0:W], data1=dummy[:, 0:W], initial=init,
                    op0=ALU.mult, op1=ALU.bypass), r=[("ps_z", zb, e), "dummy"] + prev, w=[("I", e, c)])
                sc.op("pool", lambda eh, e=e, rg=rg: eh.tensor_tensor(
                    out=wb[b][e][:, k0:512], in0=rg[:, lo:lo + W], in1=rg[:, lo + 1:lo + 1 + W], op=ALU.subtract),
                    r=[("I", e, c)] + prev, w=[("w", b, e)])

        def S2(it):
            n, r, c = it["n"], it["r"], it["c"]
            b = n % NB
            t = n % 2
            i0 = r if c == 0 else 0
            for e in range(2):
                for i in range(i0, 4):
                    sc.op("pe", lambda eh, e=e, i=i: eh.matmul(
                        ps_t[e][:, i * 128:(i + 1) * 128], lhsT=wb[b][e][:, i * 128:(i + 1) * 128], rhs=idb[:, :],
                        start=True, stop=True), r=[("w", b, e), "idb"], w=[("ps_t", e)])
            for e in range(2):
                sc.op("act", lambda eh, e=e: eh.activation(out=wT[t][e][:, i0 * 128:512], in_=ps_t[e][:, i0 * 128:512],
                                                           func=AF.Copy), r=[("ps_t", e)], w=[("wT", t, e)])

        def S3(it):
            n, hp, j, r, c, nch = it["n"], it["hp"], it["j"], it["r"], it["c"], it["nch"]
            kb = it["hi"] % 2
            t = n % 2
            ob = it["q"] % 2
            kt = 2 * j + c
            i0 = r if c == 0 else 0
            for e in range(2):
                for i in range(i0, 4):
                    sc.op("pe", lambda eh, e=e, i=i: eh.matmul(
                        ps_o[ob][:, e * 128:(e + 1) * 128], lhsT=V[kb][:, kt * 4 + i, :],
                        rhs=wT[t][e][:, i * 128:(i + 1) * 128], start=(e == 0 and c == 0 and i == i0),
                        stop=(c == nch - 1 and i == 3), skip_group_check=True),
                        r=[("V", kb), ("wT", t, e)], w=[("ps_o", ob)])
            if c == nch - 1:
                osl = (it["q"] // 4) % 2
                for e in range(2):
                    sc.op("act", lambda eh, e=e: eh.activation(
                        out=o_sb[osl][e * 64:(e + 1) * 64, r * 128:(r + 1) * 128],
                        in_=ps_o[ob][e * 64:(e + 1) * 64, e * 128:(e + 1) * 128], func=AF.Copy),
                        r=[("ps_o", ob)], w=[("o_sb", osl)])
                if r == 3:
                    sc.dma("ar", osl, 2, [(T["AR"][:, hp, 512 * j:512 * j + 512], o_sb[osl][:, :])], r=[("o_sb", osl)])

        load_kv(hp_list[0])
        for n in range(N + 3):
            if 0 <= n - 3 < N and steps[n - 3]["first_hp"] and steps[n - 3]["hi"] + 1 < len(hp_list):
                nh = steps[n - 3]["hi"] + 1
                sc.dma("kv", nh % 2, 2, [(KT[nh % 2][:, :], T["KT"][:, hp_list[nh], :]),
                                         (V[nh % 2][:, :, :], T["V"][:, hp_list[nh], :, :])],
                       w=[("KT", nh % 2), ("V", nh % 2)])
            if n < N:
                S0(steps[n])
            if 0 <= n - 1 < N:
                S1(steps[n - 1])
            if 0 <= n - 2 < N:
                S2(steps[n - 2])
            if 0 <= n - 3 < N:
                S3(steps[n - 3])
        return sc.emit()


def phase_outproj(nc, T, cf, dbg=None):
    tag = "E"
    NT = 512
    with ExitStack() as st:
        sc = Sched(nc, st, tag)
        sb = lambda n, shp, dt: st.enter_context(nc.sbuf_tensor(f"{tag}_{n}", shp, dt))
        pp = lambda n, shp, dt=F32: st.enter_context(nc.psum_tensor(f"{tag}_p_{n}", shp, dt))
        womix = sb("womix", [128, 8 * D], BF16)
        stg = [sb(f"stg{i}", [128, 1024], F32) for i in range(2)]
        ones = sb("ones", [128, 128], BF16)
        bd32 = sb("bd32", [128, 128], F32)
        epsb = sb("epsb", [128, 1], F32)
        gat = sb("gat", [128, 4], F32)
        selv = sb("selv", [128, 4], F32)
        ar2 = [sb(f"ar{i}", [128, 4, NT], F32) for i in range(2)]
        asq = sb("asq", [128, 4, NT], F32)
        rr = sb("rr", [128, NT], F32)
        cat2 = [sb(f"cat{i}", [128, 8, NT], BF16) for i in range(2)]
        h1L2 = [sb(f"h1L{i}", [128, 8, NT], F32) for i in range(2)]
        h1H2 = [sb(f"h1H{i}", [128, 8, NT], F32) for i in range(2)]
        yT = sb("yT", [128, 8, NT], F32)
        sq = sb("sq", [128, 8, NT], BF16)
        t322 = [sb(f"t32{i}", [128, 8, NT], F32) for i in range(2)]
        rs = sb("rs", [128, NT], F32)
        ps_n = [pp(f"n{i}", [128, 512]) for i in range(2)]
        ps_y = [pp(f"y{i}", [128, 512]) for i in range(2)]
        ps_ss = pp("ss", [128, 512])
        _EPS["rms"] = epsb[:, 0:1]
        sc.op("pool", lambda eh: eh.memset(ones[:, :], 1.0), w=["ones"])
        sc.op("pool", lambda eh: eh.memset(epsb[:, :], RMS_EPS), w=["eps"])
        sc.dma("c", 0, 1, [(bd32[:, :], T["bd"][:, :]), (gat[:, :], T["gattn"][:, :]), (selv[:, :], T["selv"][:, :])],
               w=["bd32", "gat", "selv"])
        for i_ in range(2):
            a0, b0 = i_ * 4096, (i_ + 1) * 4096
            sc.dma("wld", i_, 2, [(womix[:, a0:b0], T["womix_bf"][:, a0:b0])], w=["womix"])
        cB = lambda k: cf[:, (3 * 1 + 2) * 8 + k:(3 * 1 + 2) * 8 + k + 1]
        def load_in(j):
            q = j % 2
            sc.dma("in", q, 2, [(ar2[q][:, :, :], T["AR"][:, :, NT * j:NT * j + NT]),
                                (cat2[q][:, 4:8, :], T["U2T"][:, :, NT * j:NT * j + NT]),
                                (h1L2[q][:, :, :], T["h1"][:, :, 1024 * j:1024 * j + 512]),
                                (h1H2[q][:, :, :], T["h1"][:, :, 1024 * j + 512:1024 * j + 1024])],
                   w=[("ar", q), ("catu", q), ("h1L", q), ("h1H", q)])

        load_in(0)
        for j in range(8):
            par = j % 2
            q = j % 2
            sL = selv[:, 2 * par:2 * par + 1]
            sH = selv[:, 2 * par + 1:2 * par + 2]
            ar, cat, h1L, h1H, t32 = ar2[q], cat2[q], h1L2[q], h1H2[q], t322[q]
            if j + 1 < 8:
                load_in(j + 1)
            sc.op("act", lambda eh, ar=ar: eh.activation(out=asq[:, :, :], in_=ar[:, :, :], func=AF.Square), r=[("ar", q)], w=["asq"])
            for hp in range(4):
                g = hp % 2
                sc.op("pe", lambda eh, g=g, hp=hp: eh.matmul(ps_n[g][:, :], lhsT=bd32[:, :], rhs=asq[:, hp, :],
                                                             start=True, stop=True), r=["bd32", "asq"], w=[("ps_n", g)])
                sc.op("act", lambda eh, g=g: eh.activation(out=rr[:, :], in_=ps_n[g][:, :], func=AF.Sqrt,
                                                           bias=epsb[:, 0:1], scale=1.0), r=[("ps_n", g), "eps"], w=["rr"])
                sc.op("dve", lambda eh: eh.reciprocal(out=rr[:, :], in_=rr[:, :]), r=["rr"], w=["rr"])
                sc.op("dve", lambda eh, hp=hp, cat=cat, ar=ar: eh.scalar_tensor_tensor(
                    out=cat[:, hp, :], in0=ar[:, hp, :], scalar=gat[:, hp:hp + 1], in1=rr[:, :], op0=ALU.mult, op1=ALU.mult),
                    r=[("ar", q), "gat", "rr"], w=[("cata", q)])
            for d in range(8):
                g = d % 2
                for k in range(8):
                    sc.op("pe", lambda eh, g=g, d=d, k=k, cat=cat: eh.matmul(
                        ps_y[g][:, :], lhsT=womix[:, k * D + d * 128:k * D + d * 128 + 128], rhs=cat[:, k, :],
                        start=(k == 0), stop=(k == 7)), r=["womix", ("cata", q), ("catu", q)], w=[("ps_y", g)])
                sc.op("act", lambda eh, g=g, d=d: eh.activation(out=yT[:, d, :], in_=ps_y[g][:, :], func=AF.Copy),
                      r=[("ps_y", g)], w=["yT"])
            _norm_stats(sc, yT, "yT", sq, ones, ps_ss, rs, "rs", 8, NT, 1.0 / D, RMS_EPS)
            sc.op("act", lambda eh, sL=sL, h1L=h1L: eh.activation(out=h1L[:, :, :], in_=h1L[:, :, :], func=AF.Copy, scale=sL),
                  r=[("h1L", q), "selv"], w=[("h1L", q)])
            sc.op("dve", lambda eh, sH=sH, h1L=h1L, h1H=h1H: eh.scalar_tensor_tensor(
                out=h1L[:, :, :], in0=h1H[:, :, :], scalar=sH, in1=h1L[:, :, :], op0=ALU.mult, op1=ALU.add),
                r=[("h1L", q), ("h1H", q), "selv"], w=[("h1L", q)])
            for k in range(8):
                sc.op("dve", lambda eh, k=k, t32=t32: eh.scalar_tensor_tensor(
                    out=t32[:, k, :], in0=yT[:, k, :], scalar=cB(k), in1=rs[:, :], op0=ALU.mult, op1=ALU.mult),
                    r=["yT", "rs", "cf"], w=[("t32", q, k)])
                sc.op("pool", lambda eh, k=k, t32=t32, h1L=h1L: eh.tensor_tensor(
                    out=t32[:, k, :], in0=t32[:, k, :], in1=h1L[:, k, :], op=ALU.add),
                    r=[("t32", q, k), ("h1L", q)], w=[("t32", q, k)])
            sc.dma("h2", q, 2, [(T["h2"][:, :, NT * j:NT * j + NT], t32[:, :, :])], r=[("t32", q, k) for k in range(8)])
        return sc.emit()


def build_program(stop_after=5, debug=False):
    nc = bass.Bass("TRN2", target_bir_lowering=False)
    din = lambda n, shp, dt=F32: nc.dram_tensor(n, shp, dt, kind="ExternalInput").ap()
    dsc = lambda n, shp, dt: nc.dram_tensor(n, shp, dt, kind="Internal").ap()
    dout = lambda n, shp, dt=F32: nc.dram_tensor(n, shp, dt, kind="ExternalOutput").ap()
    T = {}
    T["xT"] = din("xT", [128, 8, S])
    T["cT"] = din("cT", [128, 8])
    T["wada"] = din("wada", [72, 128, 1024])
    T["bada"] = din("bada", [128, 72])
    T["gvec"] = din("gvec", [128, 48])
    T["w1in"] = din("w1in", [128, 8 * 2 * DFF])
    T["w1out"] = din("w1out", [128, FC * D])
    T["w2in"] = din("w2in", [128, 8 * 2 * DFF])
    T["w2out"] = din("w2out", [128, FC * D])
    T["wmix"] = din("wmix", [128, 8 * 2560])
    T["womix"] = din("womix", [128, 8 * D])
    T["ident"] = din("ident", [128, 128])
    T["bd"] = din("bd", [128, 128])
    T["convw"] = din("convw", [128, 124])
    T["cvec"] = din("cvec", [128, 12])
    T["gattn"] = din("gattn", [128, 4])
    T["selv"] = din("selv", [128, 4])
    T["masks"] = din("masks", [128, 16, 512], BF16)
    T["outT"] = dout("outT", [128, 8, NOWN])
    T["h1"] = dsc("h1", [128, 8, S], F32)
    T["hm"] = dsc("hm", [128, 8, S + 32], BF16)
    T["KT"] = dsc("KT", [128, 4, S], BF16)
    T["V"] = dsc("V", [128, 4, 64, 128], BF16)
    T["QT"] = dsc("QT", [128, 4, NOWN], BF16)
    T["U2T"] = dsc("U2T", [128, 4, NOWN], BF16)
    T["AR"] = dsc("AR", [128, 4, NOWN], F32)
    T["h2"] = dsc("h2", [128, 8, NOWN], F32)
    T["w2in_bf"] = dsc("w2in_bf", [128, 8 * 2 * DFF], BF16)
    T["w2out_bf"] = dsc("w2out_bf", [128, FC * D], BF16)
    T["wmix_bf"] = dsc("wmix_bf", [128, 8 * 2560], BF16)
    T["womix_bf"] = dsc("womix_bf", [128, 8 * D], BF16)
    dbg = None
    if debug:
        dbg = {"modT": dout("dbg_modT", [128, 72])}
        dnames = {"h1": ([128, 8, S], F32), "hm": ([128, 8, S + 32], BF16), "KT": ([128, 4, S], BF16),
                  "V": ([128, 4, 64, 128], BF16), "QT": ([128, 4, NOWN], BF16), "U2T": ([128, 4, NOWN], BF16),
                  "AR": ([128, 4, NOWN], F32), "h2": ([128, 8, NOWN], F32)}
        for n, (shp, dt) in dnames.items():
            dbg[n] = dout("dbg_" + n, shp, dt)
    with nc.sbuf_tensor("cf", [128, 72], F32) as cf:
        phase_ffn(nc, T, cf, 1, dbg)
        if stop_after >= 2:
            phase_inproj(nc, T, cf, dbg)
        if stop_after >= 3:
            if debug:
                phase_attn(nc, T, dbg, hp_list=DBG_HP, slot_list=DBG_SLOTS)
            else:
                phase_attn(nc, T, dbg)
        if stop_after >= 4:
            phase_outproj(nc, T, cf, dbg)
        if stop_after >= 5:
            phase_ffn(nc, T, cf, 2, dbg)
        if debug:
            with ExitStack() as st:
                sc = Sched(nc, st, "DBG")
                for n in dnames:
                    full = tuple(slice(None) for _ in dnames[n][0])
                    sc.dma("d" + n, 0, 1, [(dbg[n][full], T[n][full])])
                sc.emit()
    return nc


WARM_PE = False
DBG_HP = (0, 1, 2, 3)
DBG_SLOTS = tuple(range(8))

def _fm(v, nchunk):
    return np.ascontiguousarray(np.asarray(v, dtype=np.float32).reshape(nchunk, 128).T)


def _wl(w, kc):
    w = np.asarray(w, dtype=np.float32)
    n = w.shape[1]
    return np.ascontiguousarray(w.reshape(kc, 128, n).transpose(1, 0, 2).reshape(128, kc * n))


def prep_shared(inp):
    sh = {}
    wa = np.asarray(inp["w_ada"], dtype=np.float32)
    sh["wada"] = np.ascontiguousarray(wa.reshape(8, 128, 72, 128).transpose(2, 1, 0, 3).reshape(72, 128, 1024))
    sh["bada"] = _fm(inp["b_ada"], 72)
    sh["gvec"] = np.ascontiguousarray(np.concatenate(
        [_fm(inp[n], 8) for n in ("g_pre_ff1", "g_post_ff1", "g_pre_mix", "g_post_mix", "g_pre_ff2", "g_post_ff2")],
        axis=1))
    sh["w1in"] = _wl(inp["ff1_w_in"], 8)
    sh["w1out"] = _wl(inp["ff1_w_out"], FC)
    sh["w2in"] = _wl(inp["ff2_w_in"], 8)
    sh["w2out"] = _wl(inp["ff2_w_out"], FC)
    return sh


def prep_core(inp, core):
    b, p = core // 2, core % 2
    x = np.asarray(inp["x"], dtype=np.float32)[b]
    xf = x[::-1, :]
    m = {}
    m["xT"] = np.ascontiguousarray(xf.T.reshape(8, 128, S).transpose(1, 0, 2))
    m["cT"] = _fm(np.asarray(inp["c"], dtype=np.float32)[b], 8)
    return m


def _own_off(p, j):
    return (j % 2) if p == 0 else 1 - (j % 2)


def prep_shared2(inp, sh):
    sh["wmix"] = _wl(inp["w_in_mix"], 8)
    sh["womix"] = _wl(inp["w_out_mix"], 8)
    sh["ident"] = np.eye(128, dtype=np.float32)
    bd = np.zeros((128, 128), np.float32)
    bd[:64, :64] = 1.0 / 64
    bd[64:, 64:] = 1.0 / 64
    sh["bd"] = bd
    cw = np.asarray(inp["conv_w"], dtype=np.float32)
    sh["convw"] = np.ascontiguousarray(cw.T.reshape(4, 128, 31).transpose(1, 0, 2).reshape(128, 124))
    sh["cvec"] = np.ascontiguousarray(np.concatenate(
        [_fm(inp["conv_b"], 4), _fm(inp["conv_ln_g"], 4), _fm(inp["conv_ln_b"], 4)], axis=1))
    sh["gattn"] = np.ascontiguousarray(np.asarray(inp["g_attn_out"], dtype=np.float32).reshape(4, 128).T)
    return sh


def prep_core2(m, core):
    p = core % 2
    selv = np.zeros((128, 4), np.float32)
    masks = np.zeros((128, 2, 2, 4, 512), np.float32)
    pp = np.arange(128)[:, None]
    jj = np.arange(512)[None, :]
    for par in range(2):
        o = _own_off(p, par)
        selv[:, 2 * par] = 1.0 if o == 0 else 0.0
        selv[:, 2 * par + 1] = 1.0 if o == 1 else 0.0
        for r in range(4):
            diag = np.where(jj <= r * 128 + pp, NEG, 0.0).astype(np.float32)
            if o == 0:
                masks[:, par, 0, r] = diag
            else:
                masks[:, par, 0, r] = NEG
                masks[:, par, 1, r] = diag
    m["selv"] = selv
    m["masks"] = np.ascontiguousarray(masks.reshape(128, 16, 512)).astype(ml_dtypes.bfloat16)
    return m


IN_NAMES = ["xT", "cT", "wada", "bada", "gvec", "w1in", "w1out", "w2in", "w2out", "wmix", "womix", "ident", "bd",
            "convw", "cvec", "gattn", "selv", "masks"]
_PROG = {}


def kernel(**inputs):
    sh = prep_shared2(inputs, prep_shared(inputs))
    in_maps = []
    for core in range(8):
        m = dict(sh)
        m.update(prep_core(inputs, core))
        prep_core2(m, core)
        in_maps.append({k: m[k] for k in IN_NAMES})
    if "nc" not in _PROG:
        _PROG["nc"] = build_program()
    res = run_bass_kernel_spmd(_PROG["nc"], in_maps, core_ids=list(range(8)))
    out = np.zeros((4, S, D), np.float32)
    for core in range(8):
        b, p = core // 2, core % 2
        oT = np.asarray(res.results[core]["outT"])
        for j in range(8):
            tile = 2 * j + _own_off(p, j)
            blk = oT[:, :, 512 * j:512 * j + 512].transpose(2, 1, 0).reshape(512, D)
            out[b, S - tile * 512 - 512:S - tile * 512, :] = blk[::-1]
    return out
```

```python
import os
import numpy as np
import ml_dtypes
from contextlib import ExitStack
import concourse.bass as bass
import concourse.mybir as mybir
from concourse.bass_utils import run_bass_kernel_spmd

F32 = mybir.dt.float32
BF16 = mybir.dt.bfloat16
AF = mybir.ActivationFunctionType
ALU = mybir.AluOpType

D = 1024
KC = 8
S = 8192
DFF = 2816
FC = 22
NOWN = 4096
TT = 256
RMS_EPS = 1e-6
LN_EPS = 1e-5
NEG = -30720.0
ENGS = ("pe", "act", "dve", "pool", "sp")
BLK = {"pe": "tensor", "act": "scalar", "dve": "vector", "pool": "gpsimd", "sp": "sync"}


class _Op:
    __slots__ = ("eng", "fn", "deps", "signal", "sig_idx", "is_dma", "dsem", "dkey", "dval")


class Sched:
    def __init__(self, nc, st, tag):
        self.nc, self.st, self.tag = nc, st, tag
        self.ops = []
        self.last_w = {}
        self.readers = {}
        self.streams = {}
        self.nd = 0

    def _sem(self, name):
        return self.st.enter_context(self.nc.semaphore(f"{self.tag}_{name}"))

    def _add(self, op, reads, writes):
        deps = op.deps
        for k in reads:
            w = self.last_w.get(k)
            if w is not None:
                deps.append((w, 0))
        for k in writes:
            w = self.last_w.get(k)
            if w is not None:
                deps.append((w, 1))
            rd = self.readers.get(k)
            if rd:
                for r in rd.values():
                    deps.append((r, 2))
        for k in writes:
            self.last_w[k] = op
            self.readers[k] = {}
        for k in reads:
            if op.is_dma:
                self.nd += 1
                rk = ("dma", self.nd)
            else:
                rk = op.eng
            self.readers.setdefault(k, {})[rk] = op
        self.ops.append(op)

    def op(self, eng, fn, r=(), w=()):
        o = _Op()
        o.eng, o.fn, o.deps, o.signal, o.sig_idx, o.is_dma = eng, fn, [], False, 0, False
        self._add(o, r, w)
        return o

    def dma(self, stream, slot, nslots, pairs, r=(), w=()):
        stt = self.streams.get(stream)
        if stt is None:
            stt = self.streams[stream] = {
                "sems": [self._sem(f"d{stream}{i}") for i in range(nslots)], "cnt": [0] * nslots}
        o = _Op()
        o.eng, o.deps, o.signal, o.sig_idx, o.is_dma = "sp", [], False, 0, True
        o.fn = lambda eh, pairs=pairs: [eh.dma_start(out=a, in_=b) for (a, b) in pairs]
        o.dsem = stt["sems"][slot]
        o.dkey = (stream, slot)
        stt["cnt"][slot] += 16 * len(pairs)
        o.dval = stt["cnt"][slot]
        self._add(o, r, w)
        return o

    @staticmethod
    def _needs(op, p, kind):
        if p.is_dma:
            return True
        if p.eng == op.eng and not op.is_dma:
            return kind == 0 and p.eng != "pe"
        return True

    def emit(self):
        for op in self.ops:
            for (p, kind) in op.deps:
                if not p.is_dma and self._needs(op, p, kind):
                    p.signal = True
        cnt = {e: 0 for e in ENGS}
        for op in self.ops:
            if not op.is_dma and op.signal:
                cnt[op.eng] += 1
                op.sig_idx = cnt[op.eng]
        esem = {e: self._sem("e" + e) for e in ENGS if cnt[e] > 0}
        per = {e: [op for op in self.ops if op.eng == e] for e in ENGS}
        needs = self._needs
        streams = self.streams
        with self.nc.Block() as block:
            for e in ENGS:
                if not per[e]:
                    continue

                def body(eh, e=e):
                    waited = {}
                    for op in per[e]:
                        need = {}
                        for (p, kind) in op.deps:
                            if not needs(op, p, kind):
                                continue
                            if p.is_dma:
                                key, sem, val = ("d",) + p.dkey, p.dsem, p.dval
                            else:
                                key, sem, val = ("e", p.eng), esem[p.eng], p.sig_idx
                            if key not in need or need[key][1] < val:
                                need[key] = (sem, val)
                        for key, (sem, val) in need.items():
                            if waited.get(key, 0) >= val:
                                continue
                            eh.wait_ge(sem, val)
                            waited[key] = val
                        ins = op.fn(eh)
                        if op.is_dma:
                            for i in ins:
                                i.then_inc(op.dsem, 16)
                        elif op.signal:
                            ins.then_inc(esem[e], 1)
                    if e == "sp":
                        for stt in streams.values():
                            for sem, c in zip(stt["sems"], stt["cnt"]):
                                if c > 0:
                                    eh.wait_ge(sem, c)

                getattr(block, BLK[e])(body)
        return cnt


def _cast(sc, i, dst, src, r, w):
    e = ("dve", "pool", "act")[i % 3]
    if e == "act":
        sc.op("act", lambda eh: eh.activation(out=dst, in_=src, func=AF.Copy), r=r, w=w)
    else:
        sc.op(e, lambda eh: eh.tensor_copy(out=dst, in_=src), r=r, w=w)


def _load_weight(sc, stg, ctr, dst, dst_key, src, total, piece=1024):
    n = (total + piece - 1) // piece
    for i in range(n):
        a, b = i * piece, min(total, (i + 1) * piece)
        sl = ctr[0] % len(stg)
        ctr[0] += 1
        sc.dma("stg", sl, len(stg), [(stg[sl][:, 0:b - a], src[:, a:b])], w=[("stg", sl)])
        _cast(sc, ctr[0], dst[:, a:b], stg[sl][:, 0:b - a], r=[("stg", sl)], w=[dst_key])


def _norm_p1(sc, src, src_key, sq, nchunk):
    sc.op("act", lambda eh: eh.activation(out=sq[:, 0:nchunk, :], in_=src[:, 0:nchunk, :], func=AF.Square),
          r=[src_key], w=["sq"])


def _norm_p2(sc, sq, ones, ps_ss, rs, rs_key, nchunk, ntok, inv_n):
    for k in range(nchunk):
        sc.op("pe", lambda eh, k=k: eh.matmul(ps_ss[:, 0:ntok], lhsT=ones[:, :], rhs=sq[:, k, :],
                                              start=(k == 0), stop=(k == nchunk - 1)),
              r=["sq", "ones"], w=["ps_ss"])
    sc.op("act", lambda eh: eh.activation(out=rs[:, :], in_=ps_ss[:, 0:ntok], func=AF.Sqrt, bias=eps_ap(sc), scale=inv_n),
          r=["ps_ss", "eps"], w=[rs_key])
    sc.op("dve", lambda eh: eh.reciprocal(out=rs[:, :], in_=rs[:, :]), r=[rs_key], w=[rs_key])


def _norm_stats(sc, src, src_key, sq, ones, ps_ss, rs, rs_key, nchunk, ntok, inv_n, eps):
    _norm_p1(sc, src, src_key, sq, nchunk)
    _norm_p2(sc, sq, ones, ps_ss, rs, rs_key, nchunk, ntok, inv_n)


_EPS = {}


def eps_ap(sc):
    return _EPS["rms"]


def phase_ffn(nc, T, cf, which, dbg=None):
    tag = f"F{which}"
    ntiles = (S if which == 1 else NOWN) // TT
    s_idx = 0 if which == 1 else 2
    src_dram = T["xT"] if which == 1 else T["h2"]
    with ExitStack() as st:
        sc = Sched(nc, st, tag)
        sb = lambda n, shp, dt: st.enter_context(nc.sbuf_tensor(f"{tag}_{n}", shp, dt))
        pp = lambda n, shp, dt=F32: st.enter_context(nc.psum_tensor(f"{tag}_p_{n}", shp, dt))
        w_in = sb("w_in", [128, 8 * 2 * DFF], BF16)
        w_out = sb("w_out", [128, FC * D], BF16)
        stg = [sb(f"stg{i}", [128, 1024], F32) for i in range(2)]
        ones = sb("ones", [128, 128], BF16)
        epsb = sb("epsb", [128, 1], F32)
        xT = [sb(f"xT{i}", [128, 8, TT], F32) for i in range(2)]
        sq = sb("sq", [128, 8, TT], BF16)
        t32 = sb("t32", [128, 8, TT], F32)
        hT = [sb(f"hT{i}", [128, 8, TT], BF16) for i in range(2)]
        sg = [sb(f"sg{i}", [128, TT], F32) for i in range(2)]
        actT = sb("actT", [128, FC, TT], BF16)
        yT = sb("yT", [128, 8, TT], F32)
        hm = sb("hm", [128, 8, TT], BF16) if which == 1 else None
        rs1 = [sb(f"rs1{i}", [128, TT], F32) for i in range(2)]
        rs2 = sb("rs2", [128, TT], F32)
        rs3 = rs2
        ps_gu = [pp(f"gu{i}", [128, 512]) for i in range(2)]
        ps_y = [pp(f"y{i}", [128, 512]) for i in range(2)]
        ps_ss = pp("ss", [128, 512])
        ps_md = pp("md", [128, 512])
        _EPS["rms"] = epsb[:, 0:1]

        sc.op("pool", lambda eh: eh.memset(ones[:, :], 1.0), w=["ones"])
        sc.op("pool", lambda eh: eh.memset(epsb[:, :], RMS_EPS), w=["eps"])
        ctr = [0]

        if which == 1:
            scT = sb("scT", [128, 8], F32)
            gv = sb("gv", [128, 48], F32)
            bada = sb("bada", [128, 72], F32)
            modT = sb("modT", [128, 72], F32)
            zpad = sb("zpad", [128, 8, 32], BF16)
            sc.dma("c", 0, 1, [(scT[:, :], T["cT"][:, :]), (gv[:, :], T["gvec"][:, :]),
                               (bada[:, :], T["bada"][:, :])], w=["scT", "gv", "bada"])
            sc.op("act", lambda eh: eh.activation(out=scT[:, :], in_=scT[:, :], func=AF.Silu), r=["scT"], w=["scT"])
            for j in range(72):
                sl = ctr[0] % 2
                ctr[0] += 1
                sc.dma("stg", sl, 2, [(stg[sl][:, :], T["wada"][j, :, :])], w=[("stg", sl)])
                for k in range(8):
                    sc.op("pe", lambda eh, sl=sl, j=j, k=k: eh.matmul(
                        ps_md[:, j:j + 1], lhsT=stg[sl][:, k * 128:(k + 1) * 128], rhs=scT[:, k:k + 1],
                        start=(k == 0), stop=(k == 7)), r=[("stg", sl), "scT"], w=["ps_md"])
            sc.op("dve", lambda eh: eh.tensor_tensor(out=modT[:, :], in0=ps_md[:, 0:72], in1=bada[:, :], op=ALU.add),
                  r=["ps_md", "bada"], w=["modT"])
            for s in range(3):
                resw = 1.0 if s == 1 else 0.5
                A = cf[:, (3 * s) * 8:(3 * s) * 8 + 8]
                SH = cf[:, (3 * s + 1) * 8:(3 * s + 1) * 8 + 8]
                B = cf[:, (3 * s + 2) * 8:(3 * s + 2) * 8 + 8]
                m_sh = modT[:, (3 * s) * 8:(3 * s) * 8 + 8]
                m_sc = modT[:, (3 * s + 1) * 8:(3 * s + 1) * 8 + 8]
                m_g = modT[:, (3 * s + 2) * 8:(3 * s + 2) * 8 + 8]
                gpre = gv[:, (2 * s) * 8:(2 * s) * 8 + 8]
                gpost = gv[:, (2 * s + 1) * 8:(2 * s + 1) * 8 + 8]
                sc.op("dve", lambda eh, A=A, m_sc=m_sc, gpre=gpre: eh.scalar_tensor_tensor(
                    out=A, in0=m_sc, scalar=1.0, in1=gpre, op0=ALU.add, op1=ALU.mult), r=["modT", "gv"], w=["cf"])
                sc.op("dve", lambda eh, SH=SH, m_sh=m_sh: eh.tensor_copy(out=SH, in_=m_sh), r=["modT"], w=["cf"])
                sc.op("dve", lambda eh, B=B, m_g=m_g, gpost=gpost: eh.scalar_tensor_tensor(
                    out=B, in0=m_g, scalar=1.0, in1=gpost, op0=ALU.add, op1=ALU.mult), r=["modT", "gv"], w=["cf"])
                sc.op("dve", lambda eh, B=B, resw=resw: eh.tensor_scalar(
                    out=B, in0=B, scalar1=resw, scalar2=None, op0=ALU.mult), r=["cf"], w=["cf"])
            sc.op("pool", lambda eh: eh.memset(zpad[:, :, :], 0.0), w=["zpad"])
            sc.dma("zp", 0, 1, [(T["hm"][:, :, S:S + 32], zpad[:, :, :])], r=["zpad"])
            if dbg is not None:
                sc.dma("dbg", 0, 1, [(dbg["modT"][:, :], modT[:, :])], r=["modT"])

        if which == 1:
            _load_weight(sc, stg, ctr, w_in, "w_in", T["w1in"], 8 * 2 * DFF)
            _load_weight(sc, stg, ctr, w_out, "w_out", T["w1out"], FC * D)
            stgb = [sb(f"stgb{i}", [128, 512], BF16) for i in range(2)]
            jobs = []
            for (src, dst, tot) in ((T["w2in"], T["w2in_bf"], 8 * 2 * DFF), (T["w2out"], T["w2out_bf"], FC * D),
                                    (T["wmix"], T["wmix_bf"], 8 * 2560), (T["womix"], T["womix_bf"], 8 * D)):
                for a0 in range(0, tot, 512):
                    jobs.append((src[:, a0:a0 + 512], dst[:, a0:a0 + 512]))
            jobs.reverse()

            pc_state = {"n": 0, "load": None, "cast": None}

            def precast_job():
                if pc_state["cast"] is not None:
                    dst, sl = pc_state["cast"]
                    sc.dma("pco", sl, 2, [(dst, stgb[sl][:, :])], r=[("stgb", sl)])
                    pc_state["cast"] = None
                if pc_state["load"] is not None:
                    dst, sl = pc_state["load"]
                    sc.op("act", lambda eh, sl=sl: eh.activation(out=stgb[sl][:, :], in_=stg[sl][:, 0:512], func=AF.Copy),
                          r=[("stg", sl)], w=[("stgb", sl)])
                    pc_state["cast"] = (dst, sl)
                    pc_state["load"] = None
                if jobs:
                    src, dst = jobs.pop()
                    sl = pc_state["n"] % 2
                    pc_state["n"] += 1
                    sc.dma("stg", sl, 2, [(stg[sl][:, 0:512], src)], w=[("stg", sl)])
                    pc_state["load"] = (dst, sl)

            def precast_pending():
                return bool(jobs) or pc_state["load"] is not None or pc_state["cast"] is not None
        else:
            for i_ in range(4):
                a0, b0 = i_ * 11264, (i_ + 1) * 11264
                sc.dma("wld", i_, 4, [(w_in[:, a0:b0], T["w2in_bf"][:, a0:b0])], w=["w_in"])
            for i_ in range(2):
                a0, b0 = i_ * 11264, (i_ + 1) * 11264
                sc.dma("wld2", i_, 2, [(w_out[:, a0:b0], T["w2out_bf"][:, a0:b0])], w=["w_out"])

            def precast_job():
                return

        cA = lambda s, k: cf[:, (3 * s) * 8 + k:(3 * s) * 8 + k + 1]
        cS = lambda s, k: cf[:, (3 * s + 1) * 8 + k:(3 * s + 1) * 8 + k + 1]
        cB = lambda s, k: cf[:, (3 * s + 2) * 8 + k:(3 * s + 2) * 8 + k + 1]

        YT = [("yT", k) for k in range(8)]

        def mod_pair(src, src_keys, rs, rs_key, s, dst, dst_key, k):
            sc.op("dve", lambda eh: eh.scalar_tensor_tensor(
                out=t32[:, k, :], in0=src[:, k, :], scalar=cA(s, k), in1=rs[:, :], op0=ALU.mult, op1=ALU.mult),
                r=[src_keys[k], rs_key, "cf"], w=[("t32", k)])
            sc.op("act", lambda eh: eh.activation(
                out=dst[:, k, :], in_=t32[:, k, :], func=AF.Identity, bias=cS(s, k), scale=1.0),
                r=[("t32", k), "cf"], w=[dst_key])

        def load_x(i):
            b = i % 2
            sc.dma("x", b, 2, [(xT[b][:, :, :], src_dram[:, :, i * TT:(i + 1) * TT])], w=[("xT", b)])

        def stage_P1(i):
            b = i % 2
            _norm_p1(sc, xT[b], ("xT", b), sq, 8)

        def stage_P2(i):
            b = i % 2
            _norm_p2(sc, sq, ones, ps_ss, rs1[b], ("rs1", b), 8, TT, 1.0 / D)

        def stage_P3(i, ks):
            b = i % 2
            for k in ks:
                mod_pair(xT[b], [("xT", b)] * 8, rs1[b], ("rs1", b), s_idx, hT[b], ("hT", b), k)

        def stage_G(i, hooks):
            b = i % 2
            for f in range(FC):
                g = f % 2
                for half in range(2):
                    col = half * DFF + f * 128
                    for k in range(8):
                        sc.op("pe", lambda eh, g=g, half=half, col=col, k=k: eh.matmul(
                            ps_gu[g][:, half * TT:(half + 1) * TT],
                            lhsT=w_in[:, k * 2 * DFF + col:k * 2 * DFF + col + 128], rhs=hT[b][:, k, :],
                            start=(k == 0), stop=(k == 7)), r=["w_in", ("hT", b)], w=[("ps_gu", g)])
                sc.op("act", lambda eh, g=g: eh.activation(out=sg[g][:, :], in_=ps_gu[g][:, 0:TT], func=AF.Silu),
                      r=[("ps_gu", g)], w=[("sg", g)])
                sc.op("dve", lambda eh, g=g, f=f: eh.tensor_tensor(
                    out=actT[:, f, :], in0=sg[g][:, :], in1=ps_gu[g][:, TT:2 * TT], op=ALU.mult),
                    r=[("sg", g), ("ps_gu", g)], w=["actT"])
                for h in hooks.get(f, ()):
                    h()

        def stage_O1(i):
            for d in range(8):
                g = d % 2
                for f in range(FC):
                    sc.op("pe", lambda eh, g=g, d=d, f=f: eh.matmul(
                        ps_y[g][:, 0:TT], lhsT=w_out[:, f * D + d * 128:f * D + d * 128 + 128], rhs=actT[:, f, :],
                        start=(f == 0), stop=(f == FC - 1)), r=["w_out", "actT"], w=[("ps_y", g)])
                sc.op("act", lambda eh, g=g, d=d: eh.activation(out=yT[:, d, :], in_=ps_y[g][:, 0:TT], func=AF.Copy),
                      r=[("ps_y", g)], w=[("yT", d)])

        def norm_p1_yT():
            sc.op("act", lambda eh: eh.activation(out=sq[:, :, :], in_=yT[:, :, :], func=AF.Square), r=YT, w=["sq"])

        def stage_O2a1(i):
            norm_p1_yT()

        def stage_O2a2(i):
            _norm_p2(sc, sq, ones, ps_ss, rs2, "rs2", 8, TT, 1.0 / D)

        def stage_O2a3(i, ks):
            b = i % 2
            for k in ks:
                sc.op("dve", lambda eh, k=k: eh.scalar_tensor_tensor(
                    out=t32[:, k, :], in0=yT[:, k, :], scalar=cB(s_idx, k), in1=rs2[:, :], op0=ALU.mult, op1=ALU.mult),
                    r=[("yT", k), "rs2", "cf"], w=[("t32", k)])
                sc.op("pool", lambda eh, k=k: eh.tensor_tensor(
                    out=yT[:, k, :], in0=t32[:, k, :], in1=xT[b][:, k, :], op=ALU.add),
                    r=[("t32", k), ("xT", b)], w=[("yT", k)])
            if 7 in ks:
                if which == 1:
                    sc.dma("h1", 0, 1, [(T["h1"][:, :, i * TT:(i + 1) * TT], yT[:, :, :])], r=YT)
                else:
                    sc.dma("out", 0, 1, [(T["outT"][:, :, i * TT:(i + 1) * TT], yT[:, :, :])], r=YT)

        def stage_O2b1(i):
            if which == 1:
                norm_p1_yT()

        def stage_O2b2(i):
            if which == 1:
                _norm_p2(sc, sq, ones, ps_ss, rs3, "rs2", 8, TT, 1.0 / D)

        def stage_O2b3(i, ks):
            if which != 1:
                return
            for k in ks:
                mod_pair(yT, YT, rs3, "rs2", 1, hm, "hm", k)
            if 7 in ks:
                sc.dma("hm", 0, 1, [(T["hm"][:, :, i * TT:(i + 1) * TT], hm[:, :, :])], r=["hm"])

        PAIRS = [(0, 1), (2, 3), (4, 5), (6, 7)]

        def tile_hooks(i):
            hooks = {}
            add = lambda f, fn: hooks.setdefault(f, []).append(fn)
            if i >= 1:
                add(0, lambda: stage_O2a1(i - 1))
                add(2, lambda: stage_O2a2(i - 1))
                for n_, ks in enumerate(PAIRS):
                    add(3 + n_, lambda ks=ks: stage_O2a3(i - 1, ks))
                add(7, lambda: stage_O2b1(i - 1))
                add(9, lambda: stage_O2b2(i - 1))
                for n_, ks in enumerate(PAIRS):
                    add(10 + n_, lambda ks=ks: stage_O2b3(i - 1, ks))
                if i + 1 < ntiles:
                    add(7, lambda: load_x(i + 1))
            for f_ in (1, 5, 8, 11, 13, 19):
                add(f_, precast_job)
            if i + 1 < ntiles:
                add(14, lambda: stage_P1(i + 1))
                add(16, lambda: stage_P2(i + 1))
                for n_, ks in enumerate(PAIRS):
                    add(17 + n_, lambda ks=ks: stage_P3(i + 1, ks))
            return hooks

        load_x(0)
        load_x(1)
        stage_P1(0)
        stage_P2(0)
        stage_P3(0, range(8))
        for w_ in range(48):
            sc.op("pe", lambda eh: eh.matmul(ps_md[:, :], lhsT=w_in[:, 512:640], rhs=w_in[:, 0:512], start=True, stop=True),
                  r=["w_in"], w=["ps_md"])
        for i in range(ntiles):
            stage_G(i, tile_hooks(i))
            stage_O1(i)
        while which == 1 and precast_pending():
            precast_job()
        last = ntiles - 1
        stage_O2a1(last)
        stage_O2a2(last)
        stage_O2a3(last, range(8))
        stage_O2b1(last)
        stage_O2b2(last)
        stage_O2b3(last, range(8))
        cnt = sc.emit()
        return cnt


def phase_inproj(nc, T, cf, dbg=None):
    tag = "B"
    NT = 512
    with ExitStack() as st:
        sc = Sched(nc, st, tag)
        sb = lambda n, shp, dt: st.enter_context(nc.sbuf_tensor(f"{tag}_{n}", shp, dt))
        pp = lambda n, shp, dt=F32: st.enter_context(nc.psum_tensor(f"{tag}_p_{n}", shp, dt))
        wmix = sb("wmix", [128, 8 * 2560], BF16)
        idf = sb("idf", [128, 128], F32)
        idb = sb("idb", [128, 128], BF16)
        ones32 = sb("ones32", [128, 128], F32)
        diag = sb("diag", [128, 124, 128], BF16)
        cw = sb("cw", [128, 124], F32)
        cvec = sb("cvec", [128, 12], F32)
        selv = sb("selv", [128, 4], F32)
        epsb = sb("epsb", [128, 1], F32)
        hmb = [sb(f"hmb{i}", [128, 8, NT], BF16) for i in range(2)]
        kt_sb = [sb(f"kt{i}", [128, 4, NT], BF16) for i in range(2)]
        v_sb = [sb(f"v{i}", [128, 4, 512], BF16) for i in range(2)]
        hmx = sb("hmx", [128, 8, 1056], BF16)
        htmp = sb("htmp", [128, 8, 544], BF16)
        hown = sb("hown", [128, 8, 544], BF16)
        qt_sb = sb("qt", [128, 4, NT], BF16)
        sgm = [sb(f"sgm{i}", [128, 544], F32) for i in range(2)]
        ub = [sb(f"u{i}", [128, 4, 544], BF16) for i in range(2)]
        cT = sb("cT", [128, 4, NT], F32)
        csq = sb("csq", [128, 4, NT], F32)
        mean_sb = sb("mean", [128, NT], F32)
        var_sb = sb("var", [128, NT], F32)
        tln = sb("tln", [128, 4, NT], F32)
        u2 = sb("u2", [128, 4, NT], BF16)
        ps_a = [pp(f"a{i}", [128, 512]) for i in range(3)]
        ps_h = [pp(f"h{i}", [128, 512]) for i in range(2)]
        ps_m = pp("m", [128, 512])
        ps_q = pp("q", [128, 512])

        sc.op("pool", lambda eh: eh.memset(ones32[:, :], 1.0 / 512.0), w=["ones32"])
        sc.op("pool", lambda eh: eh.memset(epsb[:, :], LN_EPS), w=["eps"])
        sc.dma("c", 0, 1, [(idf[:, :], T["ident"][:, :]), (cw[:, :], T["convw"][:, :]),
                           (cvec[:, :], T["cvec"][:, :]), (selv[:, :], T["selv"][:, :])],
               w=["idf", "cw", "cvec", "selv"])
        sc.op("dve", lambda eh: eh.tensor_copy(out=idb[:, :], in_=idf[:, :]), r=["idf"], w=["idb"])
        for i in range(124):
            if i % 2 == 0:
                sc.op("dve", lambda eh, i=i: eh.tensor_scalar(out=diag[:, i, :], in0=idf[:, :], scalar1=cw[:, i:i + 1],
                                                              scalar2=None, op0=ALU.mult), r=["idf", "cw"], w=["diag"])
            else:
                sc.op("act", lambda eh, i=i: eh.activation(out=diag[:, i, :], in_=idf[:, :], func=AF.Copy,
                                                           scale=cw[:, i:i + 1]), r=["idf", "cw"], w=["diag"])
        for i_ in range(4):
            a0, b0 = i_ * 5120, (i_ + 1) * 5120
            sc.dma("wld", i_, 4, [(wmix[:, a0:b0], T["wmix_bf"][:, a0:b0])], w=["wmix"])
        WM = lambda k, c0: wmix[:, k * 2560 + c0:k * 2560 + c0 + 128]
        ev = [0]

        def evac(dst, src, r, w, scale=None):
            ev[0] += 1
            if ev[0] % 2 == 0:
                sc.op("act", lambda eh: eh.activation(out=dst, in_=src, func=AF.Copy,
                                                      scale=(1.0 if scale is None else scale)), r=r, w=w)
            elif scale is None:
                sc.op("dve", lambda eh: eh.tensor_copy(out=dst, in_=src), r=r, w=w)
            else:
                sc.op("dve", lambda eh: eh.tensor_scalar(out=dst, in0=src, scalar1=scale, scalar2=None, op0=ALU.mult),
                      r=r, w=w)

        def load_hm(i):
            b = i % 2
            sc.dma("hm", b, 2, [(hmb[b][:, :, :], T["hm"][:, :, i * NT:(i + 1) * NT])], w=[("hmb", b)])

        for w_ in range(48):
            sc.op("pe", lambda eh: eh.matmul(ps_q[:, :], lhsT=wmix[:, 512:640], rhs=wmix[:, 0:512], start=True, stop=True),
                  r=["wmix"], w=["ps_q"])
        load_hm(0)
        pa = [0]
        for i in range(16):
            b = i % 2
            if i + 1 < 16:
                load_hm(i + 1)
            for hp in range(4):
                a = pa[0] % 3
                pa[0] += 1
                for k in range(8):
                    sc.op("pe", lambda eh, a=a, hp=hp, k=k, b=b: eh.matmul(
                        ps_a[a][:, :], lhsT=WM(k, 512 + hp * 128), rhs=hmb[b][:, k, :], start=(k == 0), stop=(k == 7)),
                        r=["wmix", ("hmb", b)], w=[("ps_a", a)])
                evac(kt_sb[b][:, hp, :], ps_a[a][:, :], r=[("ps_a", a)], w=[("kt", b)])
            sc.dma("kt", b, 2, [(T["KT"][:, :, i * NT:(i + 1) * NT], kt_sb[b][:, :, :])], r=[("kt", b)])
            for tb in range(4):
                a = pa[0] % 3
                pa[0] += 1
                for k in range(8):
                    sc.op("pe", lambda eh, a=a, tb=tb, k=k, b=b: eh.matmul(
                        ps_a[a][:, :], lhsT=hmb[b][:, k, tb * 128:(tb + 1) * 128],
                        rhs=wmix[:, k * 2560 + 1024:k * 2560 + 1536], start=(k == 0), stop=(k == 7)),
                        r=["wmix", ("hmb", b)], w=[("ps_a", a)])
                evac(v_sb[b][:, tb, :], ps_a[a][:, :], r=[("ps_a", a)], w=[("v", b)])
            sc.dma("v", b, 2, [(T["V"][:, hp, 4 * i:4 * i + 4, :], v_sb[b][:, :, hp * 128:(hp + 1) * 128])
                               for hp in range(4)], r=[("v", b)])

        def X(j):
            u = ub[j % 2]
            uk = ("u", j % 2)
            par = j % 2
            sL = selv[:, 2 * par:2 * par + 1]
            sH = selv[:, 2 * par + 1:2 * par + 2]
            sc.dma("hmx", 0, 1, [(hmx[:, :, :], T["hm"][:, :, 1024 * j:1024 * j + 1056])], w=["hmx"])
            sc.op("act", lambda eh, sL=sL: eh.activation(out=htmp[:, :, :], in_=hmx[:, :, 0:544], func=AF.Copy, scale=sL),
                  r=["hmx", "selv"], w=["htmp"])
            sc.op("dve", lambda eh, sH=sH: eh.scalar_tensor_tensor(
                out=hown[:, :, :], in0=hmx[:, :, 512:1056], scalar=sH, in1=htmp[:, :, :], op0=ALU.mult, op1=ALU.add),
                r=["hmx", "htmp", "selv"], w=["hown"])
            for hp in range(4):
                for k in range(8):
                    sc.op("pe", lambda eh, hp=hp, k=k: eh.matmul(
                        ps_q[:, :], lhsT=WM(k, hp * 128), rhs=hown[:, k, 0:512], start=(k == 0), stop=(k == 7)),
                        r=["wmix", "hown"], w=["ps_q"])
                evac(qt_sb[:, hp, :], ps_q[:, :], r=["ps_q"], w=["qt"], scale=0.125)
            sc.dma("qt", 0, 1, [(T["QT"][:, :, 512 * j:512 * j + 512], qt_sb[:, :, :])], r=["qt"])
            for cc in range(4):
                g = cc % 2
                for (c0, n, off) in ((0, 512, 0), (512, 32, 0)):
                    pv = ps_a[0] if n == 512 else ps_a[2]
                    pg = ps_a[1] if n == 512 else ps_a[2]
                    o2 = 0 if n == 512 else 64
                    for k in range(8):
                        sc.op("pe", lambda eh, pv=pv, cc=cc, k=k, c0=c0, n=n: eh.matmul(
                            pv[:, 0:n], lhsT=WM(k, 1536 + cc * 128), rhs=hown[:, k, c0:c0 + n],
                            start=(k == 0), stop=(k == 7)), r=["wmix", "hown"], w=[("ps_a", 0 if n == 512 else 2)])
                    for k in range(8):
                        sc.op("pe", lambda eh, pg=pg, cc=cc, k=k, c0=c0, n=n, o2=o2: eh.matmul(
                            pg[:, o2:o2 + n], lhsT=WM(k, 2048 + cc * 128), rhs=hown[:, k, c0:c0 + n],
                            start=(k == 0), stop=(k == 7)), r=["wmix", "hown"], w=[("ps_a", 1 if n == 512 else 2)])
                    sc.op("act", lambda eh, pg=pg, g=g, c0=c0, n=n, o2=o2: eh.activation(
                        out=sgm[g][:, c0:c0 + n], in_=pg[:, o2:o2 + n], func=AF.Sigmoid),
                        r=[("ps_a", 1 if n == 512 else 2)], w=[("sgm", g)])
                    sc.op("dve", lambda eh, pv=pv, g=g, cc=cc, c0=c0, n=n: eh.tensor_tensor(
                        out=u[:, cc, c0:c0 + n], in0=sgm[g][:, c0:c0 + n], in1=pv[:, 0:n], op=ALU.mult),
                        r=[("sgm", g), ("ps_a", 0 if n == 512 else 2)], w=[uk])

        def Y(j):
            u = ub[j % 2]
            uk = ("u", j % 2)
            for cc in range(4):
                h = cc % 2
                for k in range(31):
                    sc.op("pe", lambda eh, h=h, cc=cc, k=k: eh.matmul(
                        ps_h[h][:, :], lhsT=diag[:, cc * 31 + k, :], rhs=u[:, cc, 30 - k:30 - k + 512],
                        start=(k == 0), stop=(k == 30)), r=["diag", uk], w=[("ps_h", h)])
                sc.op("act", lambda eh, h=h, cc=cc: eh.activation(
                    out=cT[:, cc, :], in_=ps_h[h][:, :], func=AF.Identity, bias=cvec[:, cc:cc + 1], scale=1.0),
                    r=[("ps_h", h), "cvec"], w=["cT"])
            sc.op("act", lambda eh: eh.activation(out=csq[:, :, :], in_=cT[:, :, :], func=AF.Square), r=["cT"], w=["csq"])
            for cc in range(4):
                sc.op("pe", lambda eh, cc=cc: eh.matmul(ps_m[:, :], lhsT=ones32[:, :], rhs=cT[:, cc, :],
                                                        start=(cc == 0), stop=(cc == 3)), r=["ones32", "cT"], w=["ps_m"])
            sc.op("act", lambda eh: eh.activation(out=mean_sb[:, :], in_=ps_m[:, :], func=AF.Copy), r=["ps_m"], w=["mean"])
            for cc in range(4):
                sc.op("pe", lambda eh, cc=cc: eh.matmul(ps_q[:, :], lhsT=ones32[:, :], rhs=csq[:, cc, :],
                                                        start=(cc == 0), stop=(cc == 3)), r=["ones32", "csq"], w=["ps_q"])
            sc.op("dve", lambda eh: eh.tensor_tensor(out=var_sb[:, :], in0=mean_sb[:, :], in1=mean_sb[:, :], op=ALU.mult),
                  r=["mean"], w=["var"])
            sc.op("dve", lambda eh: eh.tensor_tensor(out=var_sb[:, :], in0=ps_q[:, :], in1=var_sb[:, :], op=ALU.subtract),
                  r=["ps_q", "var"], w=["var"])
            sc.op("act", lambda eh: eh.activation(out=var_sb[:, :], in_=var_sb[:, :], func=AF.Sqrt, bias=epsb[:, 0:1], scale=1.0),
                  r=["var", "eps"], w=["var"])
            sc.op("dve", lambda eh: eh.reciprocal(out=var_sb[:, :], in_=var_sb[:, :]), r=["var"], w=["var"])
            for cc in range(4):
                sc.op("pool", lambda eh, cc=cc: eh.tensor_tensor(out=tln[:, cc, :], in0=cT[:, cc, :], in1=mean_sb[:, :],
                                                                 op=ALU.subtract), r=["cT", "mean"], w=["tln"])
                sc.op("dve", lambda eh, cc=cc: eh.tensor_tensor(out=tln[:, cc, :], in0=tln[:, cc, :], in1=var_sb[:, :],
                                                                op=ALU.mult), r=["tln", "var"], w=["tln"])
                sc.op("act", lambda eh, cc=cc: eh.activation(
                    out=u2[:, cc, :], in_=tln[:, cc, :], func=AF.Silu, bias=cvec[:, 8 + cc:9 + cc], scale=cvec[:, 4 + cc:5 + cc]),
                    r=["tln", "cvec"], w=["u2"])
            sc.dma("u2", 0, 1, [(T["U2T"][:, :, 512 * j:512 * j + 512], u2[:, :, :])], r=["u2"])

        X(0)
        for j in range(8):
            if j + 1 < 8:
                X(j + 1)
            Y(j)
        return sc.emit()


def phase_attn(nc, T, dbg=None, hp_list=(0, 1, 2, 3), slot_list=tuple(range(8))):
    tag = "C"
    with ExitStack() as st:
        sc = Sched(nc, st, tag)
        sb = lambda n, shp, dt: st.enter_context(nc.sbuf_tensor(f"{tag}_{n}", shp, dt))
        pp = lambda n, shp, dt=F32: st.enter_context(nc.psum_tensor(f"{tag}_p_{n}", shp, dt))
        idf = sb("idf", [128, 128], F32)
        idb = sb("idb", [128, 128], BF16)
        QT = sb("QT", [128, 4, NOWN], BF16)
        msk = sb("msk", [128, 16, 512], BF16)
        KT = [sb(f"KT{i}", [128, S], BF16) for i in range(2)]
        V = [sb(f"V{i}", [128, 64, 128], BF16) for i in range(2)]
        NB = 3
        ring = [sb(f"ring{i}", [128, 1 + 16 * 512], F32) for i in range(2)]
        dummy = sb("dummy", [128, 512], F32)
        wb = [[sb(f"w{i}_{e}", [128, 512], BF16) for e in range(2)] for i in range(NB)]
        wT = [[sb(f"wT{i}_{e}", [128, 512], BF16) for e in range(2)] for i in range(2)]
        o_sb = [sb(f"o{i}", [128, 512], F32) for i in range(2)]
        ps_z = [[pp(f"z{i}_{e}", [128, 512]) for e in range(2)] for i in range(2)]
        ps_t = [pp(f"t{e}", [128, 512]) for e in range(2)]
        ps_o = [pp(f"o{i}", [128, 512]) for i in range(2)]

        sc.dma("c", 0, 1, [(idf[:, :], T["ident"][:, :]), (msk[:, :, :], T["masks"][:, :, :]),
                           (QT[:, :, :], T["QT"][:, :, :])], w=["idf", "msk", "QT"])
        sc.op("dve", lambda eh: eh.tensor_copy(out=idb[:, :], in_=idf[:, :]), r=["idf"], w=["idb"])
        for e_ in range(2):
            sc.op("pool", lambda eh, e_=e_: eh.memset(ring[e_][:, 0:1], 1.0), w=[("I0", e_)])
        sc.op("pool", lambda eh: eh.memset(dummy[:, :], 1.0), w=["dummy"])

        def load_kv(hp, b=0):
            sc.dma("kv", b, 2, [(KT[b][:, :], T["KT"][:, hp, :]), (V[b][:, :, :], T["V"][:, hp, :, :])],
                   w=[("KT", b), ("V", b)])

        steps = []
        qctr = 0
        for hi, hp in enumerate(hp_list):
            for j in slot_list:
                for r in range(4):
                    nch = 16 - 2 * j
                    for c in range(nch):
                        steps.append(dict(hp=hp, hi=hi, j=j, r=r, c=c, nch=nch, q=qctr,
                                          first_hp=(j == slot_list[0] and r == 0 and c == 0)))
                    qctr += 1
        N = len(steps)
        for n, it in enumerate(steps):
            it["n"] = n

        def geom(r, c):
            k0 = 128 * r if c == 0 else 0
            W = 512 - k0
            lo = 0 if c == 0 else (512 - 128 * r) + 512 * (c - 1)
            return k0, W, lo

        def S0(it):
            n, hp, j, r, c = it["n"], it["hp"], it["j"], it["r"], it["c"]
            kb = it["hi"] % 2
            zb = n % 2
            kt = 2 * j + c
            q0 = 512 * j + 128 * r
            k0, W, lo = geom(r, c)
            masked = c < 2
            for e in range(2):
                sc.op("pe", lambda eh, e=e: eh.matmul(
                    ps_z[zb][e][:, 0:W], lhsT=QT[e * 64:(e + 1) * 64, hp, q0:q0 + 128],
                    rhs=KT[kb][e * 64:(e + 1) * 64, kt * 512 + k0:(kt + 1) * 512], start=True, stop=(not masked)),
                    r=["QT", ("KT", kb)], w=[("ps_z", zb, e)])
            if masked:
                mi = ((j % 2) * 2 + c) * 4 + r
                for e in range(2):
                    sc.op("pe", lambda eh, e=e: eh.matmul(ps_z[zb][e][:, 0:W], lhsT=idb[:, :], rhs=msk[:, mi, k0:512],
                                                          start=False, stop=True),
                          r=["idb", "msk"], w=[("ps_z", zb, e)])
            for e in range(2):
                sc.op("act", lambda eh, e=e: eh.activation(out=ps_z[zb][e][:, 0:W], in_=ps_z[zb][e][:, 0:W], func=AF.Sigmoid,
                                                           scale=-1.0), r=[("ps_z", zb, e)], w=[("ps_z", zb, e)])

        def S1(it):
            n, r, c = it["n"], it["r"], it["c"]
            b = n % NB
            zb = n % 2
            k0, W, lo = geom(r, c)
            for e in range(2):
                rg = ring[e]
                prev = [("I", e, c - 1)] if c > 0 else [("I0", e)]
                init = 1.0 if c == 0 else rg[:, lo:lo + 1]
                sc.op("dve", lambda eh, e=e, rg=rg, init=init: eh.tensor_tensor_scan(
                    out=rg[:, lo + 1:lo + 1 + W], data0=ps_z[zb][e][:, 0:W], data1=dummy[:, 0:W], initial=init,
                    op0=ALU.mult, op1=ALU.bypass), r=[("ps_z", zb, e), "dummy"] + prev, w=[("I", e, c)])
                sc.op("pool", lambda eh, e=e, rg=rg: eh.tensor_tensor(
                    out=wb[b][e][:, k0:512], in0=rg[:, lo:lo + W], in1=rg[:, lo + 1:lo + 1 + W], op=ALU.subtract),
                    r=[("I", e, c)] + prev, w=[("w", b, e)])

        def S2(it):
            n, r, c = it["n"], it["r"], it["c"]
            b = n % NB
            t = n % 2
            i0 = r if c == 0 else 0
            for e in range(2):
                for i in range(i0, 4):
                    sc.op("pe", lambda eh, e=e, i=i: eh.matmul(
                        ps_t[e][:, i * 128:(i + 1) * 128], lhsT=wb[b][e][:, i * 128:(i + 1) * 128], rhs=idb[:, :],
                        start=True, stop=True), r=[("w", b, e), "idb"], w=[("ps_t", e)])
            for e in range(2):
                sc.op("act", lambda eh, e=e: eh.activation(out=wT[t][e][:, i0 * 128:512], in_=ps_t[e][:, i0 * 128:512],
                                                           func=AF.Copy), r=[("ps_t", e)], w=[("wT", t, e)])

        def S3(it):
            n, hp, j, r, c, nch = it["n"], it["hp"], it["j"], it["r"], it["c"], it["nch"]
            kb = it["hi"] % 2
            t = n % 2
            ob = it["q"] % 2
            kt = 2 * j + c
            i0 = r if c == 0 else 0
            for e in range(2):
                for i in range(i0, 4):
                    sc.op("pe", lambda eh, e=e, i=i: eh.matmul(
                        ps_o[ob][:, e * 128:(e + 1) * 128], lhsT=V[kb][:, kt * 4 + i, :],
                        rhs=wT[t][e][:, i * 128:(i + 1) * 128], start=(e == 0 and c == 0 and i == i0),
                        stop=(c == nch - 1 and i == 3), skip_group_check=True),
                        r=[("V", kb), ("wT", t, e)], w=[("ps_o", ob)])
            if c == nch - 1:
                osl = (it["q"] // 4) % 2
                for e in range(2):
                    sc.op("act", lambda eh, e=e: eh.activation(
                        out=o_sb[osl][e * 64:(e + 1) * 64, r * 128:(r + 1) * 128],
                        in_=ps_o[ob][e * 64:(e + 1) * 64, e * 128:(e + 1) * 128], func=AF.Copy),
                        r=[("ps_o", ob)], w=[("o_sb", osl)])
                if r == 3:
                    sc.dma("ar", osl, 2, [(T["AR"][:, hp, 512 * j:512 * j + 512], o_sb[osl][:, :])], r=[("o_sb", osl)])

        load_kv(hp_list[0])
        for n in range(N + 3):
            if 0 <= n - 3 < N and steps[n - 3]["first_hp"] and steps[n - 3]["hi"] + 1 < len(hp_list):
                nh = steps[n - 3]["hi"] + 1
                sc.dma("kv", nh % 2, 2, [(KT[nh % 2][:, :], T["KT"][:, hp_list[nh], :]),
                                         (V[nh % 2][:, :, :], T["V"][:, hp_list[nh], :, :])],
                       w=[("KT", nh % 2), ("V", nh % 2)])
            if n < N:
                S0(steps[n])
            if 0 <= n - 1 < N:
                S1(steps[n - 1])
            if 0 <= n - 2 < N:
                S2(steps[n - 2])
            if 0 <= n - 3 < N:
                S3(steps[n - 3])
        return sc.emit()


def phase_outproj(nc, T, cf, dbg=None):
    tag = "E"
    NT = 512
    with ExitStack() as st:
        sc = Sched(nc, st, tag)
        sb = lambda n, shp, dt: st.enter_context(nc.sbuf_tensor(f"{tag}_{n}", shp, dt))
        pp = lambda n, shp, dt=F32: st.enter_context(nc.psum_tensor(f"{tag}_p_{n}", shp, dt))
        womix = sb("womix", [128, 8 * D], BF16)
        stg = [sb(f"stg{i}", [128, 1024], F32) for i in range(2)]
        ones = sb("ones", [128, 128], BF16)
        bd32 = sb("bd32", [128, 128], F32)
        epsb = sb("epsb", [128, 1], F32)
        gat = sb("gat", [128, 4], F32)
        selv = sb("selv", [128, 4], F32)
        ar2 = [sb(f"ar{i}", [128, 4, NT], F32) for i in range(2)]
        asq = sb("asq", [128, 4, NT], F32)
        rr = sb("rr", [128, NT], F32)
        cat2 = [sb(f"cat{i}", [128, 8, NT], BF16) for i in range(2)]
        h1L2 = [sb(f"h1L{i}", [128, 8, NT], F32) for i in range(2)]
        h1H2 = [sb(f"h1H{i}", [128, 8, NT], F32) for i in range(2)]
        yT = sb("yT", [128, 8, NT], F32)
        sq = sb("sq", [128, 8, NT], BF16)
        t322 = [sb(f"t32{i}", [128, 8, NT], F32) for i in range(2)]
        rs = sb("rs", [128, NT], F32)
        ps_n = [pp(f"n{i}", [128, 512]) for i in range(2)]
        ps_y = [pp(f"y{i}", [128, 512]) for i in range(2)]
        ps_ss = pp("ss", [128, 512])
        _EPS["rms"] = epsb[:, 0:1]
        sc.op("pool", lambda eh: eh.memset(ones[:, :], 1.0), w=["ones"])
        sc.op("pool", lambda eh: eh.memset(epsb[:, :], RMS_EPS), w=["eps"])
        sc.dma("c", 0, 1, [(bd32[:, :], T["bd"][:, :]), (gat[:, :], T["gattn"][:, :]), (selv[:, :], T["selv"][:, :])],
               w=["bd32", "gat", "selv"])
        for i_ in range(2):
            a0, b0 = i_ * 4096, (i_ + 1) * 4096
            sc.dma("wld", i_, 2, [(womix[:, a0:b0], T["womix_bf"][:, a0:b0])], w=["womix"])
        cB = lambda k: cf[:, (3 * 1 + 2) * 8 + k:(3 * 1 + 2) * 8 + k + 1]
        for w_ in range(48):
            sc.op("pe", lambda eh: eh.matmul(ps_ss[:, :], lhsT=womix[:, 512:640], rhs=womix[:, 0:512], start=True, stop=True),
                  r=["womix"], w=["ps_ss"])
        def load_in(j):
            q = j % 2
            sc.dma("in", q, 2, [(ar2[q][:, :, :], T["AR"][:, :, NT * j:NT * j + NT]),
                                (cat2[q][:, 4:8, :], T["U2T"][:, :, NT * j:NT * j + NT]),
                                (h1L2[q][:, :, :], T["h1"][:, :, 1024 * j:1024 * j + 512]),
                                (h1H2[q][:, :, :], T["h1"][:, :, 1024 * j + 512:1024 * j + 1024])],
                   w=[("ar", q), ("catu", q), ("h1L", q), ("h1H", q)])

        load_in(0)
        for j in range(8):
            par = j % 2
            q = j % 2
            sL = selv[:, 2 * par:2 * par + 1]
            sH = selv[:, 2 * par + 1:2 * par + 2]
            ar, cat, h1L, h1H, t32 = ar2[q], cat2[q], h1L2[q], h1H2[q], t322[q]
            if j + 1 < 8:
                load_in(j + 1)
            sc.op("act", lambda eh, ar=ar: eh.activation(out=asq[:, :, :], in_=ar[:, :, :], func=AF.Square), r=[("ar", q)], w=["asq"])
            for hp in range(4):
                g = hp % 2
                sc.op("pe", lambda eh, g=g, hp=hp: eh.matmul(ps_n[g][:, :], lhsT=bd32[:, :], rhs=asq[:, hp, :],
                                                             start=True, stop=True), r=["bd32", "asq"], w=[("ps_n", g)])
                sc.op("act", lambda eh, g=g: eh.activation(out=rr[:, :], in_=ps_n[g][:, :], func=AF.Sqrt,
                                                           bias=epsb[:, 0:1], scale=1.0), r=[("ps_n", g), "eps"], w=["rr"])
                sc.op("dve", lambda eh: eh.reciprocal(out=rr[:, :], in_=rr[:, :]), r=["rr"], w=["rr"])
                sc.op("dve", lambda eh, hp=hp, cat=cat, ar=ar: eh.scalar_tensor_tensor(
                    out=cat[:, hp, :], in0=ar[:, hp, :], scalar=gat[:, hp:hp + 1], in1=rr[:, :], op0=ALU.mult, op1=ALU.mult),
                    r=[("ar", q), "gat", "rr"], w=[("cata", q)])
            for d in range(8):
                g = d % 2
                for k in range(8):
                    sc.op("pe", lambda eh, g=g, d=d, k=k, cat=cat: eh.matmul(
                        ps_y[g][:, :], lhsT=womix[:, k * D + d * 128:k * D + d * 128 + 128], rhs=cat[:, k, :],
                        start=(k == 0), stop=(k == 7)), r=["womix", ("cata", q), ("catu", q)], w=[("ps_y", g)])
                sc.op("act", lambda eh, g=g, d=d: eh.activation(out=yT[:, d, :], in_=ps_y[g][:, :], func=AF.Copy),
                      r=[("ps_y", g)], w=["yT"])
            _norm_stats(sc, yT, "yT", sq, ones, ps_ss, rs, "rs", 8, NT, 1.0 / D, RMS_EPS)
            sc.op("act", lambda eh, sL=sL, h1L=h1L: eh.activation(out=h1L[:, :, :], in_=h1L[:, :, :], func=AF.Copy, scale=sL),
                  r=[("h1L", q), "selv"], w=[("h1L", q)])
            sc.op("dve", lambda eh, sH=sH, h1L=h1L, h1H=h1H: eh.scalar_tensor_tensor(
                out=h1L[:, :, :], in0=h1H[:, :, :], scalar=sH, in1=h1L[:, :, :], op0=ALU.mult, op1=ALU.add),
                r=[("h1L", q), ("h1H", q), "selv"], w=[("h1L", q)])
            for k in range(8):
                sc.op("dve", lambda eh, k=k, t32=t32: eh.scalar_tensor_tensor(
                    out=t32[:, k, :], in0=yT[:, k, :], scalar=cB(k), in1=rs[:, :], op0=ALU.mult, op1=ALU.mult),
                    r=["yT", "rs", "cf"], w=[("t32", q, k)])
                sc.op("pool", lambda eh, k=k, t32=t32, h1L=h1L: eh.tensor_tensor(
                    out=t32[:, k, :], in0=t32[:, k, :], in1=h1L[:, k, :], op=ALU.add),
                    r=[("t32", q, k), ("h1L", q)], w=[("t32", q, k)])
            sc.dma("h2", q, 2, [(T["h2"][:, :, NT * j:NT * j + NT], t32[:, :, :])], r=[("t32", q, k) for k in range(8)])
        return sc.emit()


def build_program(stop_after=5, debug=False):
    nc = bass.Bass("TRN2", target_bir_lowering=False)
    din = lambda n, shp, dt=F32: nc.dram_tensor(n, shp, dt, kind="ExternalInput").ap()
    dsc = lambda n, shp, dt: nc.dram_tensor(n, shp, dt, kind="Internal").ap()
    dout = lambda n, shp, dt=F32: nc.dram_tensor(n, shp, dt, kind="ExternalOutput").ap()
    T = {}
    T["xT"] = din("xT", [128, 8, S])
    T["cT"] = din("cT", [128, 8])
    T["wada"] = din("wada", [72, 128, 1024])
    T["bada"] = din("bada", [128, 72])
    T["gvec"] = din("gvec", [128, 48])
    T["w1in"] = din("w1in", [128, 8 * 2 * DFF])
    T["w1out"] = din("w1out", [128, FC * D])
    T["w2in"] = din("w2in", [128, 8 * 2 * DFF])
    T["w2out"] = din("w2out", [128, FC * D])
    T["wmix"] = din("wmix", [128, 8 * 2560])
    T["womix"] = din("womix", [128, 8 * D])
    T["ident"] = din("ident", [128, 128])
    T["bd"] = din("bd", [128, 128])
    T["convw"] = din("convw", [128, 124])
    T["cvec"] = din("cvec", [128, 12])
    T["gattn"] = din("gattn", [128, 4])
    T["selv"] = din("selv", [128, 4])
    T["masks"] = din("masks", [128, 16, 512], BF16)
    T["outT"] = dout("outT", [128, 8, NOWN])
    T["h1"] = dsc("h1", [128, 8, S], F32)
    T["hm"] = dsc("hm", [128, 8, S + 32], BF16)
    T["KT"] = dsc("KT", [128, 4, S], BF16)
    T["V"] = dsc("V", [128, 4, 64, 128], BF16)
    T["QT"] = dsc("QT", [128, 4, NOWN], BF16)
    T["U2T"] = dsc("U2T", [128, 4, NOWN], BF16)
    T["AR"] = dsc("AR", [128, 4, NOWN], F32)
    T["h2"] = dsc("h2", [128, 8, NOWN], F32)
    T["w2in_bf"] = dsc("w2in_bf", [128, 8 * 2 * DFF], BF16)
    T["w2out_bf"] = dsc("w2out_bf", [128, FC * D], BF16)
    T["wmix_bf"] = dsc("wmix_bf", [128, 8 * 2560], BF16)
    T["womix_bf"] = dsc("womix_bf", [128, 8 * D], BF16)
    dbg = None
    if debug:
        dbg = {"modT": dout("dbg_modT", [128, 72])}
        dnames = {"h1": ([128, 8, S], F32), "hm": ([128, 8, S + 32], BF16), "KT": ([128, 4, S], BF16),
                  "V": ([128, 4, 64, 128], BF16), "QT": ([128, 4, NOWN], BF16), "U2T": ([128, 4, NOWN], BF16),
                  "AR": ([128, 4, NOWN], F32), "h2": ([128, 8, NOWN], F32)}
        for n, (shp, dt) in dnames.items():
            dbg[n] = dout("dbg_" + n, shp, dt)
    with nc.sbuf_tensor("cf", [128, 72], F32) as cf:
        phase_ffn(nc, T, cf, 1, dbg)
        if stop_after >= 2:
            phase_inproj(nc, T, cf, dbg)
        if stop_after >= 3:
            if debug:
                phase_attn(nc, T, dbg, hp_list=DBG_HP, slot_list=DBG_SLOTS)
            else:
                phase_attn(nc, T, dbg)
        if stop_after >= 4:
            phase_outproj(nc, T, cf, dbg)
        if stop_after >= 5:
            phase_ffn(nc, T, cf, 2, dbg)
        if debug:
            with ExitStack() as st:
                sc = Sched(nc, st, "DBG")
                for n in dnames:
                    full = tuple(slice(None) for _ in dnames[n][0])
                    sc.dma("d" + n, 0, 1, [(dbg[n][full], T[n][full])])
                sc.emit()
    return nc


WARM_PE = False
DBG_HP = (0, 1, 2, 3)
DBG_SLOTS = tuple(range(8))

def _fm(v, nchunk):
    return np.ascontiguousarray(np.asarray(v, dtype=np.float32).reshape(nchunk, 128).T)


def _wl(w, kc):
    w = np.asarray(w, dtype=np.float32)
    n = w.shape[1]
    return np.ascontiguousarray(w.reshape(kc, 128, n).transpose(1, 0, 2).reshape(128, kc * n))


def prep_shared(inp):
    sh = {}
    wa = np.asarray(inp["w_ada"], dtype=np.float32)
    sh["wada"] = np.ascontiguousarray(wa.reshape(8, 128, 72, 128).transpose(2, 1, 0, 3).reshape(72, 128, 1024))
    sh["bada"] = _fm(inp["b_ada"], 72)
    sh["gvec"] = np.ascontiguousarray(np.concatenate(
        [_fm(inp[n], 8) for n in ("g_pre_ff1", "g_post_ff1", "g_pre_mix", "g_post_mix", "g_pre_ff2", "g_post_ff2")],
        axis=1))
    sh["w1in"] = _wl(inp["ff1_w_in"], 8)
    sh["w1out"] = _wl(inp["ff1_w_out"], FC)
    sh["w2in"] = _wl(inp["ff2_w_in"], 8)
    sh["w2out"] = _wl(inp["ff2_w_out"], FC)
    return sh


def prep_core(inp, core):
    b, p = core // 2, core % 2
    x = np.asarray(inp["x"], dtype=np.float32)[b]
    xf = x[::-1, :]
    m = {}
    m["xT"] = np.ascontiguousarray(xf.T.reshape(8, 128, S).transpose(1, 0, 2))
    m["cT"] = _fm(np.asarray(inp["c"], dtype=np.float32)[b], 8)
    return m


def _own_off(p, j):
    return (j % 2) if p == 0 else 1 - (j % 2)


def prep_shared2(inp, sh):
    sh["wmix"] = _wl(inp["w_in_mix"], 8)
    sh["womix"] = _wl(inp["w_out_mix"], 8)
    sh["ident"] = np.eye(128, dtype=np.float32)
    bd = np.zeros((128, 128), np.float32)
    bd[:64, :64] = 1.0 / 64
    bd[64:, 64:] = 1.0 / 64
    sh["bd"] = bd
    cw = np.asarray(inp["conv_w"], dtype=np.float32)
    sh["convw"] = np.ascontiguousarray(cw.T.reshape(4, 128, 31).transpose(1, 0, 2).reshape(128, 124))
    sh["cvec"] = np.ascontiguousarray(np.concatenate(
        [_fm(inp["conv_b"], 4), _fm(inp["conv_ln_g"], 4), _fm(inp["conv_ln_b"], 4)], axis=1))
    sh["gattn"] = np.ascontiguousarray(np.asarray(inp["g_attn_out"], dtype=np.float32).reshape(4, 128).T)
    return sh


def prep_core2(m, core):
    p = core % 2
    selv = np.zeros((128, 4), np.float32)
    masks = np.zeros((128, 2, 2, 4, 512), np.float32)
    pp = np.arange(128)[:, None]
    jj = np.arange(512)[None, :]
    for par in range(2):
        o = _own_off(p, par)
        selv[:, 2 * par] = 1.0 if o == 0 else 0.0
        selv[:, 2 * par + 1] = 1.0 if o == 1 else 0.0
        for r in range(4):
            diag = np.where(jj <= r * 128 + pp, NEG, 0.0).astype(np.float32)
            if o == 0:
                masks[:, par, 0, r] = diag
            else:
                masks[:, par, 0, r] = NEG
                masks[:, par, 1, r] = diag
    m["selv"] = selv
    m["masks"] = np.ascontiguousarray(masks.reshape(128, 16, 512)).astype(ml_dtypes.bfloat16)
    return m


IN_NAMES = ["xT", "cT", "wada", "bada", "gvec", "w1in", "w1out", "w2in", "w2out", "wmix", "womix", "ident", "bd",
            "convw", "cvec", "gattn", "selv", "masks"]
_PROG = {}


def kernel(**inputs):
    sh = prep_shared2(inputs, prep_shared(inputs))
    in_maps = []
    for core in range(8):
        m = dict(sh)
        m.update(prep_core(inputs, core))
        prep_core2(m, core)
        in_maps.append({k: m[k] for k in IN_NAMES})
    if "nc" not in _PROG:
        _PROG["nc"] = build_program()
    res = run_bass_kernel_spmd(_PROG["nc"], in_maps, core_ids=list(range(8)))
    out = np.zeros((4, S, D), np.float32)
    for core in range(8):
        b, p = core // 2, core % 2
        oT = np.asarray(res.results[core]["outT"])
        for j in range(8):
            tile = 2 * j + _own_off(p, j)
            blk = oT[:, :, 512 * j:512 * j + 512].transpose(2, 1, 0).reshape(512, D)
            out[b, S - tile * 512 - 512:S - tile * 512, :] = blk[::-1]
    return out
```
